# Optimizing a Trainium2 kernel written in Bass

```python
import math
import jax
import jax.numpy as jnp
from jax import lax
import numpy as np

D_MODEL = 1024
BATCH = 32
SEQ = 2048
DEPTH = 4

N_META = 16
BLOCK = 128
EPS = 1e-6
D_INNER = 2 * D_MODEL
SSD_HEAD_DIM = 64
SSD_HEADS = D_INNER // SSD_HEAD_DIM
SSD_GROUPS = 8
SSD_HEADS_PER_GROUP = SSD_HEADS // SSD_GROUPS
SSD_STATE = 128
CONV_WIDTH = 4
CONV_DIM = D_INNER + 2 * SSD_GROUPS * SSD_STATE
ATTN_HEAD_DIM = 64
ATTN_Q_HEADS = D_MODEL // ATTN_HEAD_DIM
ATTN_KV_HEADS = ATTN_Q_HEADS // 4
ATTN_REP = ATTN_Q_HEADS // ATTN_KV_HEADS
D_ATTN = ATTN_Q_HEADS * ATTN_HEAD_DIM
D_KV = ATTN_KV_HEADS * ATTN_HEAD_DIM
WINDOW = 128
ROPE_THETA = 10000.0
N_BRANCH = 2
D_FF = 4 * D_MODEL
D_IN_PROJ = D_INNER + CONV_DIM + SSD_HEADS + D_ATTN + 2 * D_KV + N_BRANCH * D_MODEL

kernel_name = 'hybrid_ssd_swa_sink_gated_block'


def rms_norm(x, g):
    xf = x.astype(jnp.float32)
    xf = xf * lax.rsqrt(jnp.mean(xf * xf, axis=-1, keepdims=True) + EPS)
    return (xf * g.astype(jnp.float32)).astype(x.dtype)


def left_pad(t, pad):
    return jnp.pad(t, [(0, 0), (pad, 0)] + [(0, 0)] * (t.ndim - 2))


def segsum(a):
    t = a.shape[-1]
    cs = jnp.cumsum(a, axis=-1)
    diff = cs[..., :, None] - cs[..., None, :]
    mask = jnp.tril(jnp.ones((t, t), dtype=bool))
    return jnp.where(mask, diff, -jnp.inf)


def causal_depthwise_conv(u, w, bias):
    k, c = w.shape
    out = lax.conv_general_dilated(
        u, w[:, None, :].astype(u.dtype), window_strides=(1,), padding=[(k - 1, 0)],
        dimension_numbers=('NWC', 'WIO', 'NWC'), feature_group_count=c)
    return out + bias.astype(u.dtype)


def ssd_chunked(xs, dt, a_log, bm, cm):
    bsz, L, _, _ = xs.shape
    dtype = xs.dtype
    pad = (-L) % BLOCK
    xs, dt, bm, cm = left_pad(xs, pad), left_pad(dt, pad), left_pad(bm, pad), left_pad(cm, pad)
    nc = (L + pad) // BLOCK
    g, r, p, n = SSD_GROUPS, SSD_HEADS_PER_GROUP, SSD_HEAD_DIM, SSD_STATE
    a = -jnp.exp(a_log.astype(jnp.float32))
    adt = (dt.astype(jnp.float32) * a).reshape(bsz, nc, BLOCK, g, r)
    adt = jnp.transpose(adt, (0, 1, 3, 4, 2))
    a_cum = jnp.cumsum(adt, axis=-1)
    xdt = (xs * dt[..., None]).reshape(bsz, nc, BLOCK, g, r, p)
    bc = bm.reshape(bsz, nc, BLOCK, g, n)
    cc = cm.reshape(bsz, nc, BLOCK, g, n)
    lmat = jnp.exp(segsum(adt)).astype(dtype)
    cb = jnp.einsum('bclgn,bcsgn->bcgls', cc, bc)
    y_diag = jnp.einsum('bcgls,bcgrls,bcsgrp->bclgrp', cb, lmat, xdt)
    decay_states = jnp.exp(a_cum[..., -1:] - a_cum).astype(dtype)
    states = jnp.einsum('bclgn,bcgrl,bclgrp->bcgrpn', bc, decay_states, xdt)
    chunk_tot = jnp.pad(a_cum[..., -1], ((0, 0), (1, 0), (0, 0), (0, 0)))
    decay_chunk = jnp.exp(segsum(jnp.transpose(chunk_tot, (0, 2, 3, 1)))).astype(dtype)
    states = jnp.concatenate([jnp.zeros_like(states[:, :1]), states], axis=1)
    states = jnp.einsum('bgrzc,bcgrpn->bzgrpn', decay_chunk, states)[:, :-1]
    y_off = jnp.einsum('bclgn,bcgrpn,bcgrl->bclgrp', cc, states, jnp.exp(a_cum).astype(dtype))
    y = (y_diag + y_off).reshape(bsz, nc * BLOCK, SSD_HEADS, p)
    return y[:, pad:]


def ssd_mixer(z, xbc, dt_raw, conv_w, conv_b, dt_bias, a_log, d_skip, norm_g):
    bsz, L, _ = xbc.shape
    xbc = jax.nn.silu(causal_depthwise_conv(xbc, conv_w, conv_b))
    xs, bm, cm = jnp.split(xbc, [D_INNER, D_INNER + SSD_GROUPS * SSD_STATE], axis=-1)
    xs = xs.reshape(bsz, L, SSD_HEADS, SSD_HEAD_DIM)
    bm = bm.reshape(bsz, L, SSD_GROUPS, SSD_STATE)
    cm = cm.reshape(bsz, L, SSD_GROUPS, SSD_STATE)
    dt = jax.nn.softplus(dt_raw + dt_bias.astype(dt_raw.dtype))
    y = ssd_chunked(xs, dt, a_log, bm, cm) + xs * d_skip.astype(xs.dtype)[:, None]
    y = y.reshape(bsz, L, D_INNER) * jax.nn.silu(z)
    y = rms_norm(y.reshape(bsz, L, SSD_GROUPS, D_INNER // SSD_GROUPS),
                 norm_g.reshape(SSD_GROUPS, D_INNER // SSD_GROUPS))
    return y.reshape(bsz, L, D_INNER)


def rotary(t, pos):
    half = t.shape[-1] // 2
    inv_freq = ROPE_THETA ** (-jnp.arange(half, dtype=jnp.float32) / half)
    ang = pos.astype(jnp.float32)[:, None] * inv_freq[None, :]
    cos = jnp.cos(ang)[None, :, None, :].astype(t.dtype)
    sin = jnp.sin(ang)[None, :, None, :].astype(t.dtype)
    t1, t2 = t[..., :half], t[..., half:]
    return jnp.concatenate([t1 * cos - t2 * sin, t2 * cos + t1 * sin], axis=-1)


def sliding_window_attention(q, k, v, q_norm_g, k_norm_g, sinks):
    bsz, L = q.shape[:2]
    pos = jnp.arange(L)
    q = rotary(rms_norm(q, q_norm_g), pos)
    k = rotary(rms_norm(k, k_norm_g), pos)
    k_meta, v_meta = k[:, :N_META], v[:, :N_META]
    pad = (-L) % BLOCK
    nb = (L + pad) // BLOCK
    qb = left_pad(q, pad).reshape(bsz, nb, BLOCK, ATTN_KV_HEADS, ATTN_REP, ATTN_HEAD_DIM)
    kb = left_pad(k, pad).reshape(bsz, nb, BLOCK, ATTN_KV_HEADS, ATTN_HEAD_DIM)
    vb = left_pad(v, pad).reshape(bsz, nb, BLOCK, ATTN_KV_HEADS, ATTN_HEAD_DIM)

    def with_prev(t):
        prev = jnp.pad(t, [(0, 0), (1, 0)] + [(0, 0)] * (t.ndim - 2))[:, :-1]
        return jnp.concatenate([prev, t], axis=2)

    kb2, vb2 = with_prev(kb), with_prev(vb)
    blk_pos = (jnp.arange(nb * BLOCK) - pad).reshape(nb, BLOCK)
    k_pos = jnp.concatenate([blk_pos - BLOCK, blk_pos], axis=1)
    rel = blk_pos[:, :, None] - k_pos[:, None, :]
    band_mask = (rel >= 0) & (rel < WINDOW) & (k_pos[:, None, :] >= N_META)
    meta_mask = jnp.arange(N_META)[None, None, :] <= blk_pos[:, :, None]
    scale = ATTN_HEAD_DIM ** -0.5
    s_meta = jnp.einsum('bnqhrd,bmhd->bnhrqm', qb, k_meta).astype(jnp.float32) * scale
    s_band = jnp.einsum('bnqhrd,bnkhd->bnhrqk', qb, kb2).astype(jnp.float32) * scale
    s_meta = jnp.where(meta_mask[None, :, None, None], s_meta, -jnp.inf)
    s_band = jnp.where(band_mask[None, :, None, None], s_band, -jnp.inf)
    sink = jnp.broadcast_to(
        sinks.astype(jnp.float32).reshape(1, 1, ATTN_KV_HEADS, ATTN_REP, 1, 1),
        s_meta.shape[:-1] + (1,))
    probs = jax.nn.softmax(jnp.concatenate([s_meta, s_band, sink], axis=-1), axis=-1).astype(v.dtype)
    out = (jnp.einsum('bnhrqm,bmhd->bnqhrd', probs[..., :N_META], v_meta)
           + jnp.einsum('bnhrqk,bnkhd->bnqhrd', probs[..., N_META:N_META + 2 * BLOCK], vb2))
    return out.reshape(bsz, nb * BLOCK, D_ATTN)[:, pad:]


def setup_inputs(seed: int = 0) -> dict:
    key = jax.random.key(seed)
    ks = jax.random.split(key, 20)
    f32 = jnp.float32

    def normal(k, shape, scale):
        return jax.random.normal(k, shape, f32) * scale

    dt0 = jnp.exp(jax.random.uniform(ks[7], (DEPTH, SSD_HEADS), f32, math.log(1e-3), math.log(1e-1)))
    return {
        'x': normal(ks[0], (BATCH, SEQ, D_MODEL), 1.0),
        'meta_tokens': normal(ks[1], (N_META, D_MODEL), 1.0),
        'norm1_g': 1.0 + normal(ks[2], (DEPTH, D_MODEL), 0.02),
        'w_in': normal(ks[3], (DEPTH, D_MODEL, D_IN_PROJ), D_MODEL ** -0.5),
        'b_gate': normal(ks[4], (DEPTH, N_BRANCH * D_MODEL), 0.01),
        'conv_w': normal(ks[5], (DEPTH, CONV_WIDTH, CONV_DIM), CONV_WIDTH ** -0.5),
        'conv_b': normal(ks[6], (DEPTH, CONV_DIM), 0.01),
        'dt_bias': dt0 + jnp.log(-jnp.expm1(-dt0)),
        'a_log': jnp.log(jax.random.uniform(ks[8], (DEPTH, SSD_HEADS), f32, 1.0, 16.0)),
        'd_skip': 1.0 + normal(ks[9], (DEPTH, SSD_HEADS), 0.02),
        'ssd_norm_g': 1.0 + normal(ks[10], (DEPTH, D_INNER), 0.02),
        'q_norm_g': 1.0 + normal(ks[11], (DEPTH, ATTN_HEAD_DIM), 0.02),
        'k_norm_g': 1.0 + normal(ks[12], (DEPTH, ATTN_HEAD_DIM), 0.02),
        'sinks': normal(ks[13], (DEPTH, ATTN_Q_HEADS), 1.0),
        'w_ssd_down': normal(ks[14], (DEPTH, D_INNER, D_MODEL), D_INNER ** -0.5),
        'w_attn_down': normal(ks[15], (DEPTH, D_ATTN, D_MODEL), D_ATTN ** -0.5),
        'w_o': normal(ks[16], (DEPTH, D_MODEL, D_MODEL), D_MODEL ** -0.5),
        'norm2_g': 1.0 + normal(ks[17], (DEPTH, D_MODEL), 0.02),
        'w_mlp_up': normal(ks[18], (DEPTH, D_MODEL, D_FF), D_MODEL ** -0.5),
        'w_mlp_down': normal(ks[19], (DEPTH, D_FF, D_MODEL), D_FF ** -0.5),
    }


def reference(x, meta_tokens, norm1_g, w_in, b_gate, conv_w, conv_b, dt_bias, a_log, d_skip,
              ssd_norm_g, q_norm_g, k_norm_g, sinks, w_ssd_down, w_attn_down, w_o, norm2_g,
              w_mlp_up, w_mlp_down):
    bsz = x.shape[0]
    meta = jnp.broadcast_to(meta_tokens[None].astype(x.dtype), (bsz, N_META, D_MODEL))
    h = jnp.concatenate([meta, x], axis=1)
    L = h.shape[1]
    split_at = [D_INNER, D_INNER + CONV_DIM, D_INNER + CONV_DIM + SSD_HEADS,
                D_INNER + CONV_DIM + SSD_HEADS + D_ATTN,
                D_INNER + CONV_DIM + SSD_HEADS + D_ATTN + D_KV,
                D_INNER + CONV_DIM + SSD_HEADS + D_ATTN + 2 * D_KV]
    for l in range(DEPTH):
        xn = rms_norm(h, norm1_g[l])
        proj = xn @ w_in[l]
        z, xbc, dt_raw, q, k, v, gate_logits = jnp.split(proj, split_at, axis=-1)
        y_ssd = ssd_mixer(z, xbc, dt_raw, conv_w[l], conv_b[l], dt_bias[l], a_log[l],
                          d_skip[l], ssd_norm_g[l])
        y_attn = sliding_window_attention(
            q.reshape(bsz, L, ATTN_Q_HEADS, ATTN_HEAD_DIM),
            k.reshape(bsz, L, ATTN_KV_HEADS, ATTN_HEAD_DIM),
            v.reshape(bsz, L, ATTN_KV_HEADS, ATTN_HEAD_DIM),
            q_norm_g[l], k_norm_g[l], sinks[l])
        gates = jax.nn.sigmoid(gate_logits + b_gate[l].astype(gate_logits.dtype))
        g_ssd, g_attn = jnp.split(gates, 2, axis=-1)
        merged = g_ssd * (y_ssd @ w_ssd_down[l]) + g_attn * (y_attn @ w_attn_down[l])
        h = h + merged @ w_o[l]
        hn = rms_norm(h, norm2_g[l])
        h = h + jnp.square(jax.nn.relu(hn @ w_mlp_up[l])) @ w_mlp_down[l]
    return h[:, N_META:]
```

```python
import numpy as np
import concourse.bass as bass
import concourse.mybir as mybir
from concourse.bass_utils import run_bass_kernel_spmd

F32 = mybir.dt.float32
BF16 = mybir.dt.bfloat16
ALU = mybir.AluOpType
AF = mybir.ActivationFunctionType

D = 1024
DIN = 9760
NMETA = 16
SEQ = 2048
DEPTH = 4
EPS = 1e-6
OZ, OX, ODT, OQ, OKK, OV, OG = 0, 2048, 6144, 6176, 7200, 7456, 7712
TT = 512
NCORES = 8

PC = {}
_o = 0
for _n, _w in [("g1", 8), ("g2", 8), ("cw", 128), ("cb", 32), ("bg", 16), ("ng", 16), ("dsk", 16),
               ("qg", 1), ("kg", 1), ("sink", 8), ("dtb", 32), ("alog", 32)]:
    PC[_n] = (_o, _w)
    _o += _w
NPC = _o
NCF = 4 * 128
NPOS = NMETA + SEQ
NCB = 6 * 128 + 2 * NPOS


class Trk:
    def __init__(self):
        self.cnt = {}
        self.waited = {e: {} for e in ("pe", "act", "dve", "pool", "sp")}
        self.lastw = {}
        self.readers = {}
        self.prog = {e: [] for e in ("pe", "act", "dve", "pool", "sp")}
        self.semkeys = set(["pe", "act", "dve", "pool"])
        self.semh = {}
        self.n = 0

    def _need(self, e, ev):
        semkey, val = ev
        if self.waited[e].get(semkey, 0) >= val:
            return
        self.waited[e][semkey] = val
        self.prog[e].append(("wait", semkey, val))

    def op(self, e, fn, r=(), w=(), inc=None, dsem=None):
        self.n += 1
        for b in r:
            ev = self.lastw.get(b)
            if ev is not None:
                if ev[0] == e and e == "pe":
                    continue
                self._need(e, ev)
        for b in w:
            ev = self.lastw.get(b)
            if ev is not None and not (ev[0] == e):
                self._need(e, ev)
            rd = self.readers.get(b)
            if rd:
                for sk, val in rd.items():
                    if sk == e:
                        continue
                    self._need(e, (sk, val))
        if dsem is not None:
            self.semkeys.add(dsem)
            self.cnt[dsem] = self.cnt.get(dsem, 0) + 16
            ev = (dsem, self.cnt[dsem])
            self.prog[e].append(("dma", fn, dsem))
        else:
            if inc is None:
                inc = (e != "pe")
            if inc:
                self.cnt[e] = self.cnt.get(e, 0) + 1
                ev = (e, self.cnt[e])
                self.prog[e].append(("opi", fn, e))
            else:
                ev = (e, self.cnt.get(e, 0) + 1)
                self.prog[e].append(("op", fn, None))
        for b in w:
            self.lastw[b] = ev
            self.readers[b] = {}
        for b in r:
            d = self.readers.setdefault(b, {})
            if d.get(ev[0], 0) < ev[1]:
                d[ev[0]] = ev[1]

    def replay(self, e, eng):
        semh = self.semh
        for it in self.prog[e]:
            k = it[0]
            if k == "wait":
                eng.wait_ge(semh[it[1]], it[2])
            elif k == "dma":
                it[1](eng).then_inc(semh[it[2]], 16)
            elif k == "opi":
                it[1](eng).then_inc(semh[it[2]], 1)
            else:
                it[1](eng)


def build_program(nseq, layers, prep=True, dbg=False):
    nc = bass.Bass("TRN2", target_bir_lowering=False)
    NL = len(layers)
    dr = lambda name, shape, dt, kind: nc.dram_tensor(name, shape, dt, kind=kind).ap()
    x_d = dr("x", [nseq, SEQ, D], F32, "ExternalInput")
    meta_d = dr("meta", [NMETA, D], F32, "ExternalInput")
    par_d = dr("params", [DEPTH, 128, NPC], F32, "ExternalInput")
    cf_d = dr("cf32", [128, NCF], F32, "ExternalInput")
    cb_d = dr("cbf", [128, NCB], F32, "ExternalInput")
    wsrc = {
        "in": dr("w_in", [DEPTH, D, DIN], F32, "ExternalInput"),
        "sd": dr("w_sd", [DEPTH, 2048, D], F32, "ExternalInput"),
        "ad": dr("w_ad", [DEPTH, D, D], F32, "ExternalInput"),
        "o": dr("w_o", [DEPTH, D, D], F32, "ExternalInput"),
        "up": dr("w_up", [DEPTH, D, 4096], F32, "ExternalInput"),
        "down": dr("w_down", [DEPTH, 4096, D], F32, "ExternalInput"),
    }
    wbf = {k: dr("wb_" + k, list(v.shape), BF16, "Internal") for k, v in wsrc.items()}
    out_d = dr("out", [nseq, SEQ, D], F32, "ExternalOutput")
    smeta_d = dr("smeta", [len(layers), 128, 2048], F32, "Internal")

    T = Trk()
    sb = {}
    dbgs = {}
    import contextlib
    es = contextlib.ExitStack()

    def SB(name, shape, dt):
        t = es.enter_context(nc.sbuf_tensor(name, shape, dt))
        sb[name] = t
        return t

    hT = SB("hT", [128, 8, TT], F32)
    xnT = SB("xnT", [128, 8, TT], BF16)
    big = SB("big", [128, 32 * TT], BF16)
    yT = SB("yT", [128, 16, TT], BF16)
    scr = SB("scr", [128, 31 * 1024], mybir.dt.uint8)
    fz = SB("fz", [128, 16], F32)
    dtT = SB("dtT", [128, 4, 32], F32)
    adtT = SB("adtT", [128, 4, 32], F32)
    Sst = [SB("S%d" % i, [128, 2048], F32) for i in range(NL)]
    tails = [SB("tail%d" % i, [128, 32, 3], F32) for i in range(NL)]
    tails_m = [SB("tailm%d" % i, [128, 32, 3], F32) for i in range(NL)]
    kprev_t = [SB("kpt%d" % i, [128, 4, 128], BF16) for i in range(NL)]
    kprev_b = [SB("kpb%d" % i, [128, 4, 128], BF16) for i in range(NL)]
    kmeta_t = [SB("kmt%d" % i, [128, 4, 16], BF16) for i in range(NL)]
    kmeta_b = [SB("kmb%d" % i, [128, 4, 16], BF16) for i in range(NL)]
    vprev = [SB("vp%d" % i, [128, 256], BF16) for i in range(NL)]
    vmeta = [SB("vm%d" % i, [16, 256], BF16) for i in range(NL)]
    NRING = 4
    ring = [SB("ring%d" % i, [128, 4096], BF16) for i in range(NRING)]
    cf = SB("cf", [128, NCF], F32)
    cb = SB("cb", [128, NCB], BF16)
    par = SB("par", [128, DEPTH, NPC], F32)
    abc = SB("abc", [128, DEPTH, 32], F32)
    esk = SB("esk", [128, DEPTH, 8], F32)
    ps = [es.enter_context(nc.psum_tensor("ps%d" % i, [128, 512], F32)) for i in range(8)]

    tri_f, strict_f, ones_f, ident_f = (cf[:, i * 128:(i + 1) * 128] for i in range(4))
    tri_b, strict_b, ones_b, ident_b, bones_b, prot_b = (cb[:, i * 128:(i + 1) * 128] for i in range(6))
    cosT = cb[:, 768:768 + NPOS]
    sinT = cb[:, 768 + NPOS:768 + 2 * NPOS]

    def P(l, name):
        o, w = PC[name]
        return par[:, l, o:o + w]

    xsT = big[:, 0:16 * TT].rearrange("p (c t) -> p c t", c=16)
    BT = big[:, 16 * TT:24 * TT].rearrange("p (c t) -> p c t", c=8)
    CT = big[:, 24 * TT:32 * TT].rearrange("p (c t) -> p c t", c=8)
    qT = big[:, 0:8 * TT].rearrange("p (c t) -> p c t", c=8)
    kc_t = big[:, 8 * TT:12 * TT].rearrange("p (c t) -> p c t", c=4)
    kc_b = big[:, 12 * TT:16 * TT].rearrange("p (c t) -> p c t", c=4)
    Vt = big[:, 16 * TT:18 * TT].rearrange("p (b c) -> p b c", b=4)
    yaT = big[:, 18 * TT:26 * TT].rearrange("p (c t) -> p c t", c=8)
    hid = big[:, :].rearrange("p (c t) -> p c t", c=32)

    class Carve:
        def __init__(self):
            self.o = 0

        def get(self, nbytes, dt, shape=None):
            a = scr[:, self.o:self.o + nbytes].bitcast(dt)
            self.o += nbytes
            assert self.o <= 31 * 1024, self.o
            return a

    c = Carve()
    sq = c.get(8 * TT * 2, BF16).rearrange("p (c t) -> p c t", c=8)
    rstd = c.get(TT * 4, F32)
    ubuf = [c.get(520 * 4, F32) for _ in range(2)]
    accb = [c.get(TT * 4, F32) for _ in range(2)]
    stage = [c.get(1024 * 4, F32) for _ in range(2)]
    t3b = [c.get(TT * 4, F32) for _ in range(2)]
    c = Carve()
    xdt = c.get(4096, BF16)
    xdtd = c.get(4096, BF16)
    Btm = c.get(2048, BF16)
    CBm = c.get(2048, BF16).rearrange("p (g l) -> p g l", g=8)
    rseg2 = [c.get(2048, F32).rearrange("p (r l) -> p r l", r=4) for _ in range(2)]
    Lt2 = [c.get(1024, BF16).rearrange("p (r l) -> p r l", r=4) for _ in range(2)]
    Ebc2 = [c.get(1024, BF16).rearrange("p (r l) -> p r l", r=4) for _ in range(2)]
    Mt2 = [c.get(1024, BF16).rearrange("p (r l) -> p r l", r=4) for _ in range(2)]
    Ce2 = [c.get(1024, BF16).rearrange("p (r l) -> p r l", r=4) for _ in range(2)]
    S_bf = c.get(4096, BF16)
    dsc = c.get(128, F32)
    etot = c.get(128, F32)
    c = Carve()
    zs = [c.get(TT * 2, BF16) for _ in range(2)]
    sqy = [c.get(TT * 2, BF16) for _ in range(4)]
    rg2 = [c.get(TT * 4, F32) for _ in range(2)]
    c = Carve()
    qf2 = [c.get(TT * 4, F32) for _ in range(2)]
    qsq2 = [c.get(TT * 2, BF16) for _ in range(2)]
    qr2 = [c.get(TT * 4, F32) for _ in range(2)]
    qn2 = [c.get(TT * 2, BF16) for _ in range(2)]
    t12 = [c.get(TT * 4, F32) for _ in range(2)]
    t22 = [c.get(TT * 4, F32) for _ in range(2)]
    Pown2 = [c.get(1024, BF16) for _ in range(2)]
    Pprev2 = [c.get(1024, BF16) for _ in range(2)]
    Pmeta2 = [c.get(1024, BF16) for _ in range(2)]
    den2 = [c.get(1024, F32) for _ in range(2)]
    c = Carve()
    mT = c.get(8 * TT * 2, BF16).rearrange("p (c t) -> p c t", c=8)
    gs_sb = c.get(4 * TT * 2, BF16).rearrange("p (c t) -> p c t", c=4)
    ga_sb = c.get(4 * TT * 2, BF16).rearrange("p (c t) -> p c t", c=4)
    m1 = c.get(TT * 4, F32)
    m2 = c.get(TT * 4, F32)
    c = Carve()
    _ = c.get(8 * TT * 2 + TT * 4, BF16)
    rl = [c.get(TT * 4, F32) for _ in range(2)]

    ALLSCR = ([("sq", i) for i in range(8)] + ["rstd"] + [(n, i) for n in ("u", "acc", "stage", "xdt", "xdtd", "CBm", "zs",
              "rl", "rg") for i in range(2)] + [("sqy", i) for i in range(4)] + ["Btm", "S_bf", "dsc", "etot", "m1", "m2"] +
              [(n, i) for n in ("qf", "qsq", "qr", "qn", "t1", "t2", "rseg", "Lt", "Ebc", "Mt", "Ce", "den", "t3", ("P", "own"), ("P", "prev"), ("P", "meta"))
               for i in range(2)] +
              [("mT", i) for i in range(8)] + [("gs", i) for i in range(4)] + [("ga", i) for i in range(4)])

    BIGA = [("xsT", j) for j in range(16)] + [("BT", j) for j in range(8)] + [("CT", j) for j in range(8)]
    BIGB = [("qT", j) for j in range(8)] + ["kc_t", "kc_b"] + [("Vt", j) for j in range(4)] + \
           [("yaT", j) for j in range(8)]
    BIGC = [("hid", j) for j in range(32)]

    def mm(out, lhsT, rhs, start, stop, r, w, inc=None, tp=None):
        if inc is None:
            inc = stop
        if tp is None:
            T.op("pe", lambda e: e.matmul(out, lhsT=lhsT, rhs=rhs, start=start, stop=stop), r=r, w=w, inc=inc)
        else:
            T.op("pe", lambda e: e.matmul(out, lhsT=lhsT, rhs=rhs, start=start, stop=stop, tile_position=tp),
                 r=r, w=w, inc=inc)

    def tr(out, in_, ident, r, w, inc=False):
        T.op("pe", lambda e: e.transpose(out, in_, ident), r=r, w=w, inc=inc)

    def act(out, in_, func, r, w, bias=None, scale=None):
        kw = {}
        if bias is not None:
            kw["bias"] = bias
        if scale is not None:
            kw["scale"] = scale
        T.op("act", lambda e: e.activation(out=out, in_=in_, func=func, **kw), r=r, w=w)

    def tt(eng, out, in0, in1, op, r, w):
        T.op(eng, lambda e: e.tensor_tensor(out=out, in0=in0, in1=in1, op=op), r=r, w=w)

    def ts(eng, out, in0, s1, op0, r, w, s2=None, op1=None):
        if op1 is None:
            T.op(eng, lambda e: e.tensor_scalar(out=out, in0=in0, scalar1=s1, scalar2=None, op0=op0), r=r, w=w)
        else:
            T.op(eng, lambda e: e.tensor_scalar(out=out, in0=in0, scalar1=s1, scalar2=s2, op0=op0, op1=op1),
                 r=r, w=w)

    def stt(eng, out, in0, scalar, in1, op0, op1, r, w):
        T.op(eng, lambda e: e.scalar_tensor_tensor(out=out, in0=in0, scalar=scalar, in1=in1, op0=op0, op1=op1),
             r=r, w=w)

    def cp(eng, out, in_, r, w):
        if eng == "act":
            act(out, in_, AF.Copy, r, w)
        else:
            T.op(eng, lambda e: e.tensor_copy(out=out, in_=in_), r=r, w=w)

    def dma(out, in_, r, w, dsem, eng="sp"):
        T.op(eng, lambda e: e.dma_start(out=out, in_=in_), r=r, w=w, dsem=dsem)

    def dump(name, ap, rkeys):
        if not dbg or name in dbgs:
            return
        shp = list(ap.shape)
        d = nc.dram_tensor("dbg_" + name, shp, F32, kind="ExternalOutput").ap()
        dbgs[name] = d
        dma(d, ap, r=list(rkeys), w=[("dbg", name)], dsem="d_dbg", eng="pool")

    dma(cf[:, :], cf_d[:, :], r=[], w=["cf"], dsem="d_const")
    dma(par[:, :, :], par_d.rearrange("l p c -> p l c"), r=[], w=["par"], dsem="d_const2")
    dma(cb[:, :], cb_d[:, :], r=[], w=["cb"], dsem="d_cb", eng="pool")
    CONST = ["cf", "cb", "par", "derived"]
    for li in range(DEPTH):
        o, w_ = PC["alog"]
        act(abc[:, li, :], par[:, li, o:o + w_], AF.Exp, r=["par"], w=["derived"])
        o, w_ = PC["sink"]
        act(esk[:, li, :], par[:, li, o:o + w_], AF.Exp, r=["par"], w=["derived"])
    ts("dve", abc[:, :, :], abc[:, :, :], -1.0, ALU.mult, r=["derived"], w=["derived"])
    for i in range(NL):
        for nm, buf in (("kpt", kprev_t), ("kpb", kprev_b), ("kmt", kmeta_t), ("kmb", kmeta_b)):
            T.op("pool", lambda e, b=buf[i]: e.memset(b[:, :, :], 0.0), r=[], w=[(nm, i)])

    prep_sem = {}
    if prep:
        for li_, l in enumerate(layers):
            key = "d_prep%d" % li_
            for name in ("in", "sd", "ad", "o", "up", "down"):
                src, dst = wsrc[name], wbf[name]
                rows = src.shape[1]
                for r0 in range(0, rows, 128):
                    dma(dst[l, r0:r0 + 128, :], src[l, r0:r0 + 128, :], r=[], w=[("wbf", l)], dsem=key, eng="pool")

    def wsched(l):
        s = []
        s += [("in", l, OX + 512 * i, 512) for i in range(8)]
        s += [("in", l, ODT, 32)]
        s += [("in", l, OZ + 512 * i, 512) for i in range(4)]
        s += [("in", l, OQ + 512 * i, 512) for i in range(2)]
        s += [("in", l, OKK, 256), ("in", l, OV, 256)]
        for h in range(2):
            s += [("in", l, OG + 512 * h, 512), ("in", l, OG + 1024 + 512 * h, 512), ("ad", l, h), ("sd", l, 2 * h),
                  ("sd", l, 2 * h + 1)]
        s += [("o", l, h) for h in range(2)]
        s += [("up", l, i) for i in range(8)]
        s += [("down", l, i) for i in range(8)]
        return s

    tiles = []
    for s_ in range(nseq):
        if s_ == 0:
            tiles.append((s_, -1))
        for t_ in range(4):
            tiles.append((s_, t_))
    gsched = []
    for (s_, t_) in tiles:
        for l in layers:
            gsched += wsched(l)
    wstate = {"issued": 0, "next": 0}

    def w_issue(k):
        d = gsched[k]
        slot = k % NRING
        rb = ring[slot]
        name, l = d[0], d[1]
        if name == "in":
            c0, ncol = d[2], d[3]
            dst = rb[:, 0:8 * ncol].rearrange("p (k f) -> p k f", k=8)
            src = wbf["in"][l].rearrange("(k p) f -> p k f", p=128)[:, :, c0:c0 + ncol]
        elif name == "sd":
            dst = rb[:, :].rearrange("p (k f) -> p k f", k=16)
            src = wbf["sd"][l].rearrange("(k p) f -> p k f", p=128)[:, :, d[2] * 256:(d[2] + 1) * 256]
        elif name in ("ad", "o"):
            dst = rb[:, :].rearrange("p (k f) -> p k f", k=8)
            src = wbf[name][l].rearrange("(k p) f -> p k f", p=128)[:, :, d[2] * 512:(d[2] + 1) * 512]
        elif name == "up":
            dst = rb[:, :].rearrange("p (k f) -> p k f", k=8)
            src = wbf["up"][l].rearrange("(k p) f -> p k f", p=128)[:, :, d[2] * 512:(d[2] + 1) * 512]
        else:
            dst = rb[:, :].rearrange("p (k f) -> p k f", k=32)
            src = wbf["down"][l].rearrange("(k p) f -> p k f", p=128)[:, :, d[2] * 128:(d[2] + 1) * 128]
        dma(dst, src, r=[("wbf", l)], w=[("ring", slot)], dsem="d_ring%d" % slot)

    held = set()
    released = set()

    def wget(expect, hold=False):
        k = wstate["next"]
        assert gsched[k] == expect, (gsched[k], expect)
        for m in range(max(0, k - NRING - 1), k):
            if m not in held:
                released.add(m)
        if hold:
            held.add(k)
        while wstate["issued"] < min(len(gsched), k + NRING):
            m = wstate["issued"]
            if m >= NRING and (m - NRING) not in released:
                break
            w_issue(m)
            wstate["issued"] += 1
        assert wstate["issued"] > k, ("weight ring overflow", k, expect)
        wstate["next"] = k + 1
        slot = k % NRING
        return ring[slot], ("ring", slot)

    def wunhold():
        held.clear()

    psrot = {"i": 0}
    DENSE_BANKS = [0, 1, 4, 5, 6, 7]

    def dense_ps():
        k = psrot["i"]
        psrot["i"] = (k + 1) % len(DENSE_BANKS)
        i = DENSE_BANKS[k]
        return ps[i], ("ps", i)

    def rsq(out, in_ps, n, rk, wk_):
        act(out, in_ps, AF.Ln, r=rk, w=[wk_], bias=EPS, scale=1.0 / n)
        act(out, out, AF.Exp, r=[wk_], w=[wk_], scale=-0.5)

    def rmsnorm_tile(l, gname, Tn):
        g = P(l, gname)
        for kc in range(8):
            act(sq[:, kc, :Tn], hT[:, kc, :Tn], AF.Square, r=["hT"], w=[("sq", kc)])
        pt, pk = dense_ps()
        for kc in range(8):
            mm(pt[:, :Tn], ones_b, sq[:, kc, :Tn], kc == 0, kc == 7, r=[("sq", kc), "cb"], w=[pk])
        rsq(rstd[:, :Tn], pt[:, :Tn], D, [pk], "rstd")
        for kc in range(8):
            stt("dve", xnT[:, kc, :Tn], hT[:, kc, :Tn], g[:, kc:kc + 1], rstd[:, :Tn], ALU.mult, ALU.mult,
                r=["hT", "rstd", "par"], w=[("xnT", kc)])

    XN = [("xnT", kc) for kc in range(8)]

    def proj_chunk(wt, wk, col, Tn, m0=0, m1_=128, out_ps=None, tp=None, okey=None):
        wv = wt
        if out_ps is None:
            pt, pk = dense_ps()
            o = pt[:, :Tn]
        else:
            o, pk = out_ps, okey
        for kc in range(8):
            mm(o, wv[:, kc, col + m0:col + m1_], xnT[:, kc, :Tn], kc == 0, kc == 7, r=[wk] + XN, w=[pk], tp=tp)
        return o, pk

    def layer(li, l, Tn, BLK, NB, is_meta, tile_t, pos0):
        first_real = (tile_t == 0)
        fence(ALLSCR + BIGC + BIGA)
        rmsnorm_tile(l, "g1", Tn)
        dd = (tile_t == 0 and li == 0)
        if dd:
            dump("xnT", xnT[:, :, :], XN)
        cw = P(l, "cw")
        cbias = P(l, "cb")
        pend = []
        for i in range(8):
            wt, wk = wget(("in", l, OX + 512 * i, 512))
            wv = wt[:, :].rearrange("p (k f) -> p k f", k=8)
            for jj in range(4):
                j = 4 * i + jj
                o, pk = proj_chunk(wv, wk, jj * 128, Tn)
                u = ubuf[j % 2]
                uk = ("u", j % 2)
                if is_meta:
                    T.op("pool", lambda e, u=u: e.memset(u[:, 0:3], 0.0), r=[], w=[uk])
                else:
                    cp("pool", u[:, 0:3], tails[li][:, j, :], r=[("tail", li)], w=[uk])
                act(u[:, 3:3 + Tn], o, AF.Copy, r=[pk], w=[uk])
                cp("pool", tails[li][:, j, :], u[:, Tn:Tn + 3], r=[uk], w=[("tail", li)])
                a = accb[j % 2]
                ak = ("acc", j % 2)
                t3 = t3b[j % 2]
                t3k = ("t3", j % 2)
                act(t3[:, :Tn], o, AF.Copy, r=[pk, "par"], w=[t3k], scale=cw[:, 4 * j + 3:4 * j + 4])
                stt("dve", a[:, :Tn], u[:, 0:Tn], cw[:, 4 * j:4 * j + 1], t3[:, :Tn], ALU.mult, ALU.add,
                    r=[uk, t3k, "par"], w=[ak])
                for tap in range(1, 3):
                    stt("dve", a[:, :Tn], u[:, tap:tap + Tn], cw[:, 4 * j + tap:4 * j + tap + 1], a[:, :Tn],
                        ALU.mult, ALU.add, r=[uk, ak, "par"], w=[ak])
                if j < 16:
                    dst, dk = xsT[:, j, :Tn], ("xsT", j)
                elif j < 24:
                    dst, dk = BT[:, j - 16, :Tn], ("BT", j - 16)
                else:
                    dst, dk = CT[:, j - 24, :Tn], ("CT", j - 24)
                pend.append((dst, a, ak, dk, j))
                if len(pend) > 1:
                    d_, a_, ak_, dk_, j_ = pend.pop(0)
                    act(d_, a_[:, :Tn], AF.Silu, r=[ak_, "par"], w=[dk_], bias=cbias[:, j_:j_ + 1])
        while pend:
            d_, a_, ak_, dk_, j_ = pend.pop(0)
            act(d_, a_[:, :Tn], AF.Silu, r=[ak_, "par"], w=[dk_], bias=cbias[:, j_:j_ + 1])
        if is_meta:
            cp("pool", tails_m[li][:, :, :], tails[li][:, :, :], r=[("tail", li)], w=[("tailm", li)])
        wt, wk = wget(("in", l, ODT, 32))
        wv = wt[:, 0:256].rearrange("p (k f) -> p k f", k=8)
        for b in range(NB):
            pk = ("ps", 3)
            for kc in range(8):
                mm(ps[3][0:BLK, 0:32], xnT[:, kc, b * BLK:(b + 1) * BLK], wv[:, kc, :], kc == 0, kc == 7,
                   r=[wk] + XN, w=[pk])
            tt("dve", dtT[0:BLK, b, :], ps[3][0:BLK, 0:32], P(l, "dtb")[0:BLK, :], ALU.add, r=[pk, "par"],
               w=[("dt", b)])
            act(dtT[0:BLK, b, :], dtT[0:BLK, b, :], AF.Exp, r=[("dt", b)], w=[("dt", b)])
            act(dtT[0:BLK, b, :], dtT[0:BLK, b, :], AF.Ln, r=[("dt", b)], w=[("dt", b)], bias=1.0)
            tt("dve", adtT[0:BLK, b, :], dtT[0:BLK, b, :], abc[0:BLK, l, :], ALU.mult, r=[("dt", b), "derived"],
               w=[("adt", b)])
        if dd:
            dump("xsT", xsT[:, :, :], BIGA)
            dump("BT", BT[:, :, :], BIGA)
            dump("CT", CT[:, :, :], BIGA)
            dump("dtT", dtT[:, :, :], [("dt", b) for b in range(4)])
        fence(ALLSCR)
        S = Sst[li]
        SK = ("S", li)
        for b in range(NB):
            c0 = b * BLK
            blk = slice(c0, c0 + BLK)
            has_state = not is_meta
            for half in range(2):
                pk = ("ps", 2)
                pbf = ps[2][:, :].bitcast(BF16)
                for jj in range(8):
                    j = half * 8 + jj
                    tr(pbf[0:BLK, jj * 128:(jj + 1) * 128], xsT[:, j, blk], ident_b, r=[("xsT", j), "cb"], w=[pk],
                       inc=(jj == 7))
                tt("dve", xdt[0:BLK, half * 1024:(half + 1) * 1024].rearrange("p (h q) -> p h q", h=16),
                   pbf[0:BLK, 0:1024].rearrange("p (h q) -> p h q", h=16),
                   dtT[0:BLK, b, half * 16:(half + 1) * 16].unsqueeze(2).to_broadcast([BLK, 16, 64]), ALU.mult,
                   r=[pk, ("dt", b)], w=[("xdt", half)])
            pk = ("ps", 2)
            pbf = ps[2][:, :].bitcast(BF16)
            for g in range(8):
                tr(pbf[0:BLK, g * 128:(g + 1) * 128], BT[:, g, blk], ident_b, r=[("BT", g), "cb"], w=[pk],
                   inc=(g == 7))
            act(Btm[0:BLK, :], pbf[0:BLK, 0:1024], AF.Copy, r=[pk], w=["Btm"])
            pk = ("ps", 3)
            mm(ps[3][0:BLK, 0:32], strict_f[0:BLK, 0:BLK], adtT[0:BLK, b, :], True, True, r=[("adt", b), "cf"],
               w=[pk], inc=False)
            mm(ps[3][:, 32:64], ones_f[0:BLK, :], adtT[0:BLK, b, :], True, True, r=[("adt", b), "cf"], w=[pk],
               inc=True)
            act(dsc[0:BLK, :], ps[3][0:BLK, 0:32], AF.Exp, r=[pk], w=["dsc"])
            act(etot[:, :], ps[3][:, 32:64], AF.Exp, r=[pk], w=["etot"])
            for half in range(2):
                tt("pool", xdtd[0:BLK, half * 1024:(half + 1) * 1024].rearrange("p (h q) -> p h q", h=16),
                   xdt[0:BLK, half * 1024:(half + 1) * 1024].rearrange("p (h q) -> p h q", h=16),
                   dsc[0:BLK, half * 16:(half + 1) * 16].unsqueeze(2).to_broadcast([BLK, 16, 64]), ALU.mult,
                   r=[("xdt", half), "dsc"], w=[("xdtd", half)])
            for half in range(2):
                pk = ("ps", 4)
                for gg in range(4):
                    g = half * 4 + gg
                    mm(ps[4][0:BLK, gg * 128:gg * 128 + BLK], BT[:, g, blk], CT[:, g, blk], True, True,
                       r=[("BT", g), ("CT", g)], w=[pk], inc=(gg == 3))
                tt("dve", CBm[0:BLK, half * 4:half * 4 + 4, 0:BLK],
                   ps[4][0:BLK, :].rearrange("p (g l) -> p g l", g=4)[:, :, 0:BLK],
                   tri_f[0:BLK, 0:BLK].unsqueeze(1).to_broadcast([BLK, 4, BLK]), ALU.mult, r=[pk, "cf"],
                   w=[("CBm", half)])
            if has_state:
                cp("act", S_bf[:, :], S[:, :], r=[SK], w=["S_bf"])
            for q4 in range(4):
                pk = ("ps", 5)
                for gg in range(2):
                    g = q4 * 2 + gg
                    mm(ps[5][:, gg * 256:(gg + 1) * 256], Btm[0:BLK, g * 128:(g + 1) * 128],
                       xdtd[0:BLK, g * 256:(g + 1) * 256], True, True, r=["Btm", ("xdtd", g // 4)], w=[pk],
                       inc=(gg == 1))
                sl = slice(q4 * 512, (q4 + 1) * 512)
                if has_state:
                    tt("dve", S[:, sl].rearrange("p (h q) -> p h q", h=8), S[:, sl].rearrange("p (h q) -> p h q", h=8),
                       etot[:, q4 * 8:(q4 + 1) * 8].unsqueeze(2).to_broadcast([128, 8, 64]), ALU.mult,
                       r=[SK, "etot", "S_bf"], w=[SK])
                    tt("dve", S[:, sl], S[:, sl], ps[5][:, :], ALU.add, r=[SK, pk], w=[SK])
                else:
                    cp("dve", S[:, sl], ps[5][:, :], r=[pk], w=[SK])
            def G1(g):
                gp = g % 2
                rseg, Lt, Ebc, Mt, Ce = rseg2[gp], Lt2[gp], Ebc2[gp], Mt2[gp], Ce2[gp]
                krs, klt, keb, kmt, kce = ("rseg", gp), ("Lt", gp), ("Ebc", gp), ("Mt", gp), ("Ce", gp)
                i6, i7 = (6, 7) if gp == 0 else (4, 3)
                tt("pool", rseg[0:BLK, :, 0:BLK], tri_f[0:BLK, 0:BLK].unsqueeze(1).to_broadcast([BLK, 4, BLK]),
                   adtT[0:BLK, b, 4 * g:4 * g + 4].unsqueeze(2).to_broadcast([BLK, 4, BLK]), ALU.mult,
                   r=[("adt", b), "cf"], w=[krs])
                pk6, pk7 = ("ps", i6), ("ps", i7)
                for r_ in range(4):
                    mm(ps[i6][0:BLK, r_ * 128:r_ * 128 + BLK], strict_f[0:BLK, 0:BLK], rseg[0:BLK, r_, 0:BLK], True,
                       True, r=[krs, "cf"], w=[pk6], inc=(r_ == 3))
                if has_state:
                    for r_ in range(4):
                        mm(ps[i7][:, r_ * 128:r_ * 128 + BLK], ones_f[0:BLK, :], rseg[0:BLK, r_, 0:BLK], True, True,
                           r=[krs, "cf"], w=[pk7], inc=(r_ == 3))
                act(Lt[0:BLK, :, 0:BLK], ps[i6][0:BLK, :].rearrange("p (r l) -> p r l", r=4)[:, :, 0:BLK], AF.Exp,
                    r=[pk6], w=[klt])
                tt("dve", Mt[0:BLK, :, 0:BLK], Lt[0:BLK, :, 0:BLK],
                   CBm[0:BLK, g, 0:BLK].unsqueeze(1).to_broadcast([BLK, 4, BLK]), ALU.mult,
                   r=[klt, ("CBm", g // 4)], w=[kmt])
                if has_state:
                    act(Ebc[:, :, 0:BLK], ps[i7][:, :].rearrange("p (r l) -> p r l", r=4)[:, :, 0:BLK], AF.Exp,
                        r=[pk7], w=[keb])
                    tt("dve", Ce[:, :, 0:BLK], Ebc[:, :, 0:BLK],
                       CT[:, g, blk].unsqueeze(1).to_broadcast([128, 4, BLK]), ALU.mult, r=[keb, ("CT", g)],
                       w=[kce])

            def G2(g):
                gp = g % 2
                Mt, Ce = Mt2[gp], Ce2[gp]
                kmt, kce = ("Mt", gp), ("Ce", gp)
                pi = g % 2
                pyk = ("ps", pi)
                for r_ in range(4):
                    h = 4 * g + r_
                    half = r_ % 2
                    jj = r_ // 2
                    o = ps[pi][half * 64:(half + 1) * 64, jj * 128:jj * 128 + BLK]
                    tp = (0, 64) if half else None
                    mm(o, xdt[0:BLK, h * 64:(h + 1) * 64], Mt[0:BLK, r_, 0:BLK], True, not has_state,
                       r=[("xdt", h // 16), kmt], w=[pyk], inc=(r_ == 3 and not has_state), tp=tp)
                    if has_state:
                        mm(o, S_bf[:, h * 64:(h + 1) * 64], Ce[:, r_, 0:BLK], False, True, r=["S_bf", kce], w=[pyk],
                           inc=(r_ == 3), tp=tp)
                for jj in range(2):
                    j = 2 * g + jj
                    stt("dve", yT[:, j, blk], xsT[:, j, blk], P(l, "dsk")[:, j:j + 1],
                        ps[pi][:, jj * 128:jj * 128 + BLK], ALU.mult, ALU.add, r=[("xsT", j), "par", pyk],
                        w=[("yT", j)])

            G1(0)
            for g in range(8):
                if g + 1 < 8:
                    G1(g + 1)
                G2(g)
        if dd:
            dump("y0", yT[:, :, :], [("yT", j) for j in range(16)])
            dump("S", S[:, :], [SK])
        if is_meta:
            dma(smeta_d[li], S[:, :], r=[SK], w=[("smeta", li)], dsem="d_smo%d" % li)
        fence(ALLSCR)
        def zB2(jq):
            for pp in range(2):
                j0 = 4 * jq + 2 * pp
                pi_ = 2 + pp
                mm(ps[pi_][:, :Tn], ones_b, sqy[2 * pp][:, :Tn], True, False, r=[("sqy", 2 * pp), "cb"],
                   w=[("ps", pi_)])
                mm(ps[pi_][:, :Tn], ones_b, sqy[2 * pp + 1][:, :Tn], False, True, r=[("sqy", 2 * pp + 1), "cb"],
                   w=[("ps", pi_)])
            for pp in range(2):
                act(rg2[pp][:, :Tn], ps[2 + pp][:, :Tn], AF.Ln, r=[("ps", 2 + pp)], w=[("rg", pp)], bias=EPS,
                    scale=1.0 / 256)
            for pp in range(2):
                act(rg2[pp][:, :Tn], rg2[pp][:, :Tn], AF.Exp, r=[("rg", pp)], w=[("rg", pp)], scale=-0.5)
            for pp in range(2):
                j0 = 4 * jq + 2 * pp
                for j2 in (j0, j0 + 1):
                    stt("dve", yT[:, j2, :Tn], yT[:, j2, :Tn], P(l, "ng")[:, j2:j2 + 1], rg2[pp][:, :Tn], ALU.mult,
                        ALU.mult, r=[("yT", j2), ("rg", pp), "par"], w=[("yT", j2)])

        for i in range(4):
            wt, wk = wget(("in", l, OZ + 512 * i, 512))
            wv = wt[:, :].rearrange("p (k f) -> p k f", k=8)
            for jj in range(4):
                j = 4 * i + jj
                o, pk = proj_chunk(wv, wk, jj * 128, Tn)
                z_ = zs[j % 2]
                act(z_[:, :Tn], o, AF.Silu, r=[pk], w=[("zs", j % 2)])
                tt("pool", yT[:, j, :Tn], yT[:, j, :Tn], z_[:, :Tn], ALU.mult, r=[("yT", j), ("zs", j % 2)],
                   w=[("yT", j)])
                tt("pool", sqy[jj][:, :Tn], yT[:, j, :Tn], yT[:, j, :Tn], ALU.mult, r=[("yT", j)], w=[("sqy", jj)])
            zB2(i)
        if dd:
            dump("yn", yT[:, :, :], [("yT", j) for j in range(16)])
        fence(ALLSCR + BIGA + BIGB)
        if not is_meta:
            T.op("pool", lambda e: e.memset(kc_t[64:128, :, :], 0.0), r=[], w=["kc_t"])
            T.op("pool", lambda e: e.memset(kc_b[0:64, :, :], 0.0), r=[], w=["kc_b"])
        posl = slice(pos0, pos0 + Tn)

        nrc = {"i": 0}
        nr_pend = []

        def nr_s1(o, pk, gcol, dsts):
            i = nrc["i"] % 2
            nrc["i"] += 1
            qf, qsq, qr = qf2[i], qsq2[i], qr2[i]
            act(qf[:, :Tn], o, AF.Copy, r=[pk], w=[("qf", i)])
            tt("pool", qsq[:, :Tn], qf[:, :Tn], qf[:, :Tn], ALU.mult, r=[("qf", i)], w=[("qsq", i)])
            mm(ps[2][:, :Tn], bones_b, qsq[:, :Tn], True, True, r=[("qsq", i), "cb"], w=[("ps", 2)])
            rsq(qr[:, :Tn], ps[2][:, :Tn], 64, [("ps", 2)], ("qr", i))
            nr_pend.append((i, gcol, dsts))

        def nr_s2():
            i, gcol, dsts = nr_pend.pop(0)
            qf, qr, qn, t1, t2 = qf2[i], qr2[i], qn2[i], t12[i], t22[i]
            stt("dve", qn[:, :Tn], qf[:, :Tn], gcol, qr[:, :Tn], ALU.mult, ALU.mult,
                r=[("qf", i), ("qr", i), "par"], w=[("qn", i)])
            mm(ps[3][:, :Tn], prot_b, qn[:, :Tn], True, True, r=[("qn", i), "cb"], w=[("ps", 3)])
            tt("pool", t1[:, :Tn], qn[:, :Tn], cosT[:, posl], ALU.mult, r=[("qn", i), "cb"], w=[("t1", i)])
            tt("dve", t2[:, :Tn], ps[3][:, :Tn], sinT[:, posl], ALU.mult, r=[("ps", 3), "cb"], w=[("t2", i)])
            for (psl, dst, dk) in dsts:
                tt("pool", dst, t1[psl, :Tn], t2[psl, :Tn], ALU.add, r=[("t1", i), ("t2", i)], w=[dk])

        def norm_rope(o, pk, gcol, dsts):
            nr_s1(o, pk, gcol, dsts)
            if len(nr_pend) > 1:
                nr_s2()

        for i in range(2):
            wt, wk = wget(("in", l, OQ + 512 * i, 512))
            wv = wt[:, :].rearrange("p (k f) -> p k f", k=8)
            for jj in range(4):
                cq = 4 * i + jj
                o, pk = proj_chunk(wv, wk, jj * 128, Tn)
                norm_rope(o, pk, P(l, "qg")[:, 0:1], [(slice(0, 128), qT[:, cq, :Tn], ("qT", cq))])
        wt, wk = wget(("in", l, OKK, 256))
        wv = wt[:, 0:2048].rearrange("p (k f) -> p k f", k=8)
        for j in range(4):
            pt, pk = dense_ps()
            for kc in range(8):
                mm(pt[0:64, :Tn], wv[:, kc, j * 64:(j + 1) * 64], xnT[:, kc, :Tn], kc == 0, kc == 7, r=[wk] + XN,
                   w=[pk], inc=False)
            for kc in range(8):
                mm(pt[64:128, :Tn], wv[:, kc, j * 64:(j + 1) * 64], xnT[:, kc, :Tn], kc == 0, kc == 7, r=[wk] + XN,
                   w=[pk], inc=(kc == 7), tp=(0, 64))
            if is_meta:
                d_t, d_b = kmeta_t[li][0:64, j, 0:Tn], kmeta_b[li][64:128, j, 0:Tn]
                kt_, kb_ = ("kmt", li), ("kmb", li)
            else:
                d_t, d_b = kc_t[0:64, j, :Tn], kc_b[64:128, j, :Tn]
                kt_, kb_ = "kc_t", "kc_b"
            norm_rope(pt[:, :Tn], pk, P(l, "kg")[:, 0:1], [(slice(0, 64), d_t, kt_), (slice(64, 128), d_b, kb_)])
        while nr_pend:
            nr_s2()
        wt, wk = wget(("in", l, OV, 256))
        wv = wt[:, 0:2048].rearrange("p (k f) -> p k f", k=8)
        for b in range(NB):
            pt, pk = dense_ps()
            for kc in range(8):
                mm(pt[0:BLK, 0:256], xnT[:, kc, b * BLK:(b + 1) * BLK], wv[:, kc, :], kc == 0, kc == 7,
                   r=[wk] + XN, w=[pk])
            if is_meta:
                act(vmeta[li][0:BLK, :], pt[0:BLK, 0:256], AF.Copy, r=[pk], w=[("vm", li)])
            else:
                act(Vt[0:BLK, b, :], pt[0:BLK, 0:256], AF.Copy, r=[pk], w=[("Vt", b)])
        if dd:
            dump("qT", qT[:, :, :], BIGB)
            dump("kc_t", kc_t[:, :, :], BIGB)
            dump("kc_b", kc_b[:, :, :], BIGB)
            dump("Vt", Vt[:, :, :], BIGB)
        QK = [("qT", cq) for cq in range(8)]
        def cq_(ap2d):
            return ap2d.rearrange("p (c q) -> p c q", c=2)[:, :, 0:BLK]

        def A1(b, j):
            blk = slice(b * BLK, (b + 1) * BLK)
            jp_ = j % 2
            Pown, Pprev, Pmeta, den = Pown2[jp_], Pprev2[jp_], Pmeta2[jp_], den2[jp_]
            io, ipv, ime, i7 = (4, 5, 6, 7) if jp_ == 0 else (1, 2, 3, 0)
            parts = []
            if is_meta:
                parts.append(("own", io, BLK, kmeta_t[li][:, j, 0:BLK], kmeta_b[li][:, j, 0:BLK],
                              vmeta[li][0:BLK, j * 64:(j + 1) * 64], [("kmt", li), ("kmb", li), ("vm", li)], Pown,
                              tri_b))
            else:
                parts.append(("own", io, BLK, kc_t[:, j, blk], kc_b[:, j, blk], Vt[0:BLK, b, j * 64:(j + 1) * 64],
                              ["kc_t", "kc_b", ("Vt", b)], Pown, tri_b))
                if b > 0:
                    pb = slice((b - 1) * BLK, b * BLK)
                    parts.append(("prev", ipv, BLK, kc_t[:, j, pb], kc_b[:, j, pb],
                                  Vt[0:BLK, b - 1, j * 64:(j + 1) * 64], ["kc_t", "kc_b", ("Vt", b - 1)], Pprev,
                                  strict_b))
                elif not first_real:
                    parts.append(("prev", ipv, 128, kprev_t[li][:, j, :], kprev_b[li][:, j, :],
                                  vprev[li][:, j * 64:(j + 1) * 64], [("kpt", li), ("kpb", li), ("vp", li)], Pprev,
                                  strict_b))
                parts.append(("meta", ime, 16, kmeta_t[li][:, j, :], kmeta_b[li][:, j, :],
                              vmeta[li][0:16, j * 64:(j + 1) * 64], [("kmt", li), ("kmb", li), ("vm", li)], Pmeta,
                              None))
            rhs_q = qT[:, 2 * j:2 * j + 2, blk]
            for (nm, pi, KR, lt_, lb_, vap, keys, Pb, mask) in parts:
                pk = ("ps", pi)
                mm(cq_(ps[pi][0:KR, 0:256]), lt_, rhs_q, True, True, r=keys + QK, w=[pk], inc=False)
                mm(cq_(ps[pi][0:KR, 256:512]), lb_, rhs_q, True, True, r=keys + QK, w=[pk], inc=True)
                pv = ps[pi][0:KR, :].rearrange("p (a q) -> p a q", a=4)[:, :, 0:BLK]
                Pm = Pb[0:KR, :].rearrange("p (a q) -> p a q", a=4)[:, :, 0:BLK]
                act(Pm, pv, AF.Exp, r=[pk], w=[(("P", nm), jp_)], scale=0.125)
                if mask is not None:
                    tt("pool", Pm, Pm, mask[0:KR, 0:BLK].unsqueeze(1).to_broadcast([KR, 4, BLK]), ALU.mult,
                       r=[(("P", nm), jp_), "cb"], w=[(("P", nm), jp_)])
            return (b, j, blk, jp_, parts, den, i7)

        def A2(ctx):
            (b, j, blk, jp_, parts, den, i7) = ctx
            pk7 = ("ps", i7)
            npart = len(parts)
            for s in range(2):
                tp = (0, 64) if s else None
                for ip, (nm, pi, KR, lt_, lb_, vap, keys, Pb, mask) in enumerate(parts):
                    rhs = cq_(Pb[0:KR, s * 256:(s + 1) * 256])
                    mm(cq_(ps[i7][s * 64:(s + 1) * 64, 0:256]), vap, rhs, ip == 0, ip == npart - 1,
                       r=keys + [(("P", nm), jp_)], w=[pk7], inc=False, tp=tp)
                for ip, (nm, pi, KR, lt_, lb_, vap, keys, Pb, mask) in enumerate(parts):
                    rhs = cq_(Pb[0:KR, s * 256:(s + 1) * 256])
                    mm(cq_(ps[i7][s * 64:(s + 1) * 64, 256:512]), ones_b[0:KR, 0:64], rhs, ip == 0,
                       ip == npart - 1, r=[(("P", nm), jp_), "cb"], w=[pk7], inc=(s == 1 and ip == npart - 1), tp=tp)
            dv = cq_(den[:, 0:256])
            tt("dve", dv, cq_(ps[i7][:, 256:512]),
               esk[:, l, 2 * j:2 * j + 2].unsqueeze(2).to_broadcast([128, 2, BLK]), ALU.add,
               r=[pk7, "derived"], w=[("den", jp_)])
            T.op("dve", lambda e, dv=dv: e.reciprocal(out=dv, in_=dv), r=[("den", jp_)], w=[("den", jp_)])
            tt("dve", yaT[:, 2 * j:2 * j + 2, blk], cq_(ps[i7][:, 0:256]), dv,
               ALU.mult, r=[pk7, ("den", jp_)], w=[("yaT", 2 * j), ("yaT", 2 * j + 1)])

        items = [(b, j) for b in range(NB) for j in range(4)]
        ctxs = []
        for it in items:
            ctxs.append(A1(*it))
            if len(ctxs) > 1:
                A2(ctxs.pop(0))
        while ctxs:
            A2(ctxs.pop(0))
        if not is_meta:
            cp("pool", kprev_t[li][0:64, :, :], kc_t[0:64, :, Tn - 128:Tn], r=["kc_t"], w=[("kpt", li)])
            cp("pool", kprev_b[li][64:128, :, :], kc_b[64:128, :, Tn - 128:Tn], r=["kc_b"], w=[("kpb", li)])
            cp("pool", vprev[li][:, :], Vt[:, NB - 1, :], r=[("Vt", NB - 1)], w=[("vp", li)])
        if dd:
            dump("yaT", yaT[:, :, :], BIGB)
        fence(ALLSCR)
        YN = [("yT", j) for j in range(16)]
        YA = [("yaT", j) for j in range(8)]
        for h in range(2):
            wgs, kgs = wget(("in", l, OG + 512 * h, 512))
            for jj in range(4):
                oc = 4 * h + jj
                o, pk = proj_chunk(wgs[:, :].rearrange("p (k f) -> p k f", k=8), kgs, jj * 128, Tn)
                act(gs_sb[:, jj, :Tn], o, AF.Sigmoid, r=[pk, "par"], w=[("gs", jj)], bias=P(l, "bg")[:, oc:oc + 1])
            wga, kga = wget(("in", l, OG + 1024 + 512 * h, 512))
            for jj in range(4):
                oc = 4 * h + jj
                o, pk = proj_chunk(wga[:, :].rearrange("p (k f) -> p k f", k=8), kga, jj * 128, Tn)
                act(ga_sb[:, jj, :Tn], o, AF.Sigmoid, r=[pk, "par"], w=[("ga", jj)],
                    bias=P(l, "bg")[:, 8 + oc:9 + oc])
            wad, kad = wget(("ad", l, h), hold=True)
            wadv = wad[:, :].rearrange("p (k f) -> p k f", k=8)
            for jp in range(2):
                wsd, ksd = wget(("sd", l, 2 * h + jp))
                wsdv = wsd[:, :].rearrange("p (k f) -> p k f", k=16)
                for j2 in range(2):
                    jj = 2 * jp + j2
                    oc = 4 * h + jj
                    pa, pka = ps[2], ("ps", 2)
                    for kc in range(16):
                        mm(pa[:, :Tn], wsdv[:, kc, j2 * 128:(j2 + 1) * 128], yT[:, kc, :Tn], kc == 0, kc == 15,
                           r=[ksd] + YN, w=[pka])
                    pb_, pkb = ps[3], ("ps", 3)
                    for kc in range(8):
                        mm(pb_[:, :Tn], wadv[:, kc, jj * 128:(jj + 1) * 128], yaT[:, kc, :Tn], kc == 0, kc == 7,
                           r=[kad] + YA, w=[pkb])
                    tt("dve", m1[:, :Tn], pa[:, :Tn], gs_sb[:, jj, :Tn], ALU.mult, r=[pka, ("gs", jj)], w=["m1"])
                    tt("dve", m2[:, :Tn], pb_[:, :Tn], ga_sb[:, jj, :Tn], ALU.mult, r=[pkb, ("ga", jj)], w=["m2"])
                    tt("pool", mT[:, oc, :Tn], m1[:, :Tn], m2[:, :Tn], ALU.add, r=["m1", "m2"], w=[("mT", oc)])
            wunhold()
        if dd:
            dump("mT", mT[:, :, :], [("mT", j) for j in range(8)])
        MT = [("mT", j) for j in range(8)]
        for h in range(2):
            wt, wk = wget(("o", l, h))
            wv = wt[:, :].rearrange("p (k f) -> p k f", k=8)
            for jj in range(4):
                oc = 4 * h + jj
                pt, pk = dense_ps()
                for kc in range(8):
                    mm(pt[:, :Tn], wv[:, kc, jj * 128:(jj + 1) * 128], mT[:, kc, :Tn], kc == 0, kc == 7, r=[wk] + MT,
                       w=[pk])
                tt("dve", hT[:, oc, :Tn], hT[:, oc, :Tn], pt[:, :Tn], ALU.add, r=["hT", pk], w=["hT"])
        if dd:
            dump("h1", hT[:, :, :], ["hT"])
        fence(ALLSCR + BIGB + BIGC)
        rmsnorm_tile(l, "g2", Tn)
        for i in range(8):
            wt, wk = wget(("up", l, i))
            wv = wt[:, :].rearrange("p (k f) -> p k f", k=8)
            for jj in range(4):
                j = 4 * i + jj
                o, pk = proj_chunk(wv, wk, jj * 128, Tn)
                r_ = rl[j % 2]
                act(r_[:, :Tn], o, AF.Relu, r=[pk], w=[("rl", j % 2)])
                tt("pool", hid[:, j, :Tn], r_[:, :Tn], r_[:, :Tn], ALU.mult, r=[("rl", j % 2)], w=[("hid", j)])
        HID = [("hid", j) for j in range(32)]
        for oc in range(8):
            wt, wk = wget(("down", l, oc))
            wv = wt[:, :].rearrange("p (k f) -> p k f", k=32)
            pt, pk = dense_ps()
            for kc in range(32):
                mm(pt[:, :Tn], wv[:, kc, :], hid[:, kc, :Tn], kc == 0, kc == 31, r=[wk] + HID, w=[pk])
            tt("dve", hT[:, oc, :Tn], hT[:, oc, :Tn], pt[:, :Tn], ALU.add, r=["hT", pk], w=["hT"])


    def fence(keys):
        T.op("pool", lambda e: e.memset(fz[0:1, 0:1], 0.0), r=[], w=list(keys))

    for (s_, t_) in tiles:
        is_meta = (t_ < 0)
        if is_meta:
            Tn, BLK, NB, pos0 = NMETA, NMETA, 1, 0
        else:
            Tn, BLK, NB, pos0 = TT, 128, 4, NMETA + TT * t_
        if s_ > 0 and t_ == 0:
            for li in range(NL):
                dma(Sst[li][:, :], smeta_d[li], r=[("smeta", li)], w=[("S", li)], dsem="d_smi%d" % li)
                cp("pool", tails[li][:, :, :], tails_m[li][:, :, :], r=[("tailm", li)], w=[("tail", li)])
        fence(ALLSCR)
        for b in range(NB):
            st = stage[b % 2]
            sk = ("stage", b % 2)
            if is_meta:
                src = meta_d[:, :]
            else:
                src = x_d[s_, t_ * TT + b * 128:t_ * TT + (b + 1) * 128, :]
            dma(st[0:BLK, :], src, r=[], w=[sk], dsem="d_stage%d" % (b % 2))
            for half in range(2):
                pi = 2 + half
                for kk in range(4):
                    kc = half * 4 + kk
                    tr(ps[pi][:, kk * 128:kk * 128 + BLK], st[0:BLK, kc * 128:(kc + 1) * 128], ident_f[0:BLK, 0:BLK],
                       r=[sk, "cf"], w=[("ps", pi)], inc=(kk == 3))
                cp("act" if half else "dve", hT[:, half * 4:half * 4 + 4, b * BLK:(b + 1) * BLK],
                   ps[pi][:, :].rearrange("p (k t) -> p k t", k=4)[:, :, 0:BLK], r=[("ps", pi)], w=["hT"])
        for li, l in enumerate(layers):
            layer(li, l, Tn, BLK, NB, is_meta, t_, pos0)
        fence(ALLSCR)
        if not is_meta:
            for b in range(NB):
                st = stage[b % 2]
                sk = ("stage", b % 2)
                for half in range(2):
                    pi = 2 + half
                    for kk in range(4):
                        kc = half * 4 + kk
                        tr(ps[pi][:, kk * 128:(kk + 1) * 128], hT[:, kc, b * 128:(b + 1) * 128], ident_f, r=["hT", "cf"],
                           w=[("ps", pi)], inc=(kk == 3))
                    cp("act" if half else "dve", st[:, half * 512:(half + 1) * 512], ps[pi][:, :], r=[("ps", pi)],
                       w=[sk])
                dma(out_d[s_, t_ * TT + b * 128:t_ * TT + (b + 1) * 128, :], st[:, :], r=[sk], w=[("out", s_, t_, b)],
                    dsem="d_out%d" % (b % 2))
    for k in ("d_out0", "d_out1", "d_dbg"):
        if k in T.cnt:
            T.prog["sp"].append(("wait", k, T.cnt[k]))

    for k in sorted(T.semkeys):
        T.semh[k] = es.enter_context(nc.semaphore("s_" + k))
    with nc.Block() as block:
        @block.tensor
        def _(e):
            T.replay("pe", e)

        @block.scalar
        def _(e):
            T.replay("act", e)

        @block.vector
        def _(e):
            T.replay("dve", e)

        @block.gpsimd
        def _(e):
            T.replay("pool", e)

        @block.sync
        def _(e):
            T.replay("sp", e)
    es.close()
    nc._dbg_names = list(dbgs.keys())
    return nc, T


def host_consts():
    k = np.arange(128)
    tri = (k[:, None] <= k[None, :]).astype(np.float32)
    strict = (k[:, None] > k[None, :]).astype(np.float32)
    ones = np.ones((128, 128), np.float32)
    ident = np.eye(128, dtype=np.float32)
    bones = np.zeros((128, 128), np.float32)
    bones[:64, :64] = 1
    bones[64:, 64:] = 1
    prot = np.zeros((128, 128), np.float32)
    for blk in (0, 64):
        for d in range(32):
            prot[blk + d + 32, blk + d] = -1.0
            prot[blk + d, blk + d + 32] = 1.0
    half = 32
    inv_freq = (np.float32(10000.0) ** (-np.arange(half, dtype=np.float32) / np.float32(half))).astype(np.float32)
    pos = np.arange(NPOS).astype(np.float32)
    ang = (pos[:, None] * inv_freq[None, :]).astype(np.float32)
    cos = np.cos(ang).astype(np.float32)
    sin = np.sin(ang).astype(np.float32)
    p = np.arange(128) % 32
    cosT = cos[:, p].T
    sinT = sin[:, p].T
    cf = np.concatenate([tri, strict, ones, ident], axis=1)
    cbf = np.concatenate([tri, strict, ones, ident, bones, prot, cosT, sinT], axis=1)
    return np.ascontiguousarray(cf, np.float32), np.ascontiguousarray(cbf, np.float32)


def host_params(inp):
    par = np.zeros((DEPTH, 128, NPC), np.float32)

    def put(name, arr):
        o, w = PC[name]
        par[:, :, o:o + w] = arr

    fm = lambda a, n: a.reshape(DEPTH, n, 128).transpose(0, 2, 1)
    put("g1", fm(inp["norm1_g"], 8))
    put("g2", fm(inp["norm2_g"], 8))
    cw = inp["conv_w"].reshape(DEPTH, 4, 32, 128).transpose(0, 3, 2, 1).reshape(DEPTH, 128, 128)
    put("cw", cw)
    put("cb", fm(inp["conv_b"], 32))
    put("bg", fm(inp["b_gate"], 16))
    put("ng", fm(inp["ssd_norm_g"], 16))
    put("dsk", fm(np.repeat(inp["d_skip"], 64, axis=1), 16))
    put("qg", np.tile(inp["q_norm_g"], (1, 2))[:, :, None])
    put("kg", np.tile(inp["k_norm_g"], (1, 2))[:, :, None])
    sk = inp["sinks"].reshape(DEPTH, 8, 2)
    put("sink", np.repeat(sk.transpose(0, 2, 1), 64, axis=1))
    put("dtb", np.broadcast_to(inp["dt_bias"][:, None, :], (DEPTH, 128, 32)))
    put("alog", np.broadcast_to(inp["a_log"][:, None, :], (DEPTH, 128, 32)))
    return par


_PROG = {}


def run(inputs, nseq_per_core, ncores, layers, trace=False, dbg=False):
    inp = {k: np.asarray(v) for k, v in inputs.items()}
    key = (nseq_per_core, tuple(layers))
    if key not in _PROG:
        _PROG[key] = build_program(nseq_per_core, layers, dbg=dbg)
    nc, _ = _PROG[key]
    cf, cbf = host_consts()
    par = host_params(inp)
    x = np.ascontiguousarray(inp["x"], np.float32)
    shared = {
        "meta": np.ascontiguousarray(inp["meta_tokens"], np.float32), "params": par, "cf32": cf, "cbf": cbf,
        "w_in": np.ascontiguousarray(inp["w_in"], np.float32), "w_sd": np.ascontiguousarray(inp["w_ssd_down"], np.float32),
        "w_ad": np.ascontiguousarray(inp["w_attn_down"], np.float32), "w_o": np.ascontiguousarray(inp["w_o"], np.float32),
        "w_up": np.ascontiguousarray(inp["w_mlp_up"], np.float32),
        "w_down": np.ascontiguousarray(inp["w_mlp_down"], np.float32),
    }
    in_maps = []
    for c_ in range(ncores):
        m = dict(shared)
        m["x"] = x[c_ * nseq_per_core:(c_ + 1) * nseq_per_core]
        in_maps.append(m)
    res = run_bass_kernel_spmd(nc, in_maps, core_ids=list(range(ncores)), trace=trace)
    out = np.concatenate([np.asarray(r["out"]) for r in res.results], axis=0)
    return out.astype(np.float32), res


def kernel(**inputs):
    out, _ = run(inputs, 32 // NCORES, NCORES, list(range(DEPTH)))
    return out
```

```python
import numpy as np
import concourse.bass as bass
import concourse.mybir as mybir
from concourse.bass_utils import run_bass_kernel_spmd

F32 = mybir.dt.float32
BF16 = mybir.dt.bfloat16
ALU = mybir.AluOpType
AF = mybir.ActivationFunctionType

D = 1024
DIN = 9760
NMETA = 16
SEQ = 2048
DEPTH = 4
EPS = 1e-6
OZ, OX, ODT, OQ, OKK, OV, OG = 0, 2048, 6144, 6176, 7200, 7456, 7712
TT = 512
NCORES = 8

PC = {}
_o = 0
for _n, _w in [("g1", 8), ("g2", 8), ("cw", 128), ("cb", 32), ("bg", 16), ("ng", 16), ("dsk", 16),
               ("qg", 1), ("kg", 1), ("sink", 8), ("dtb", 32), ("alog", 32)]:
    PC[_n] = (_o, _w)
    _o += _w
NPC = _o
NCF = 4 * 128
NPOS = NMETA + SEQ
NCB = 6 * 128 + 2 * NPOS


class Trk:
    def __init__(self):
        self.cnt = {}
        self.waited = {e: {} for e in ("pe", "act", "dve", "pool", "sp")}
        self.lastw = {}
        self.readers = {}
        self.prog = {e: [] for e in ("pe", "act", "dve", "pool", "sp")}
        self.semkeys = set(["pe", "act", "dve", "pool"])
        self.semh = {}
        self.n = 0

    def _need(self, e, ev):
        semkey, val = ev
        if self.waited[e].get(semkey, 0) >= val:
            return
        self.waited[e][semkey] = val
        self.prog[e].append(("wait", semkey, val))

    def op(self, e, fn, r=(), w=(), inc=None, dsem=None):
        self.n += 1
        for b in r:
            ev = self.lastw.get(b)
            if ev is not None:
                if ev[0] == e and e == "pe":
                    continue
                self._need(e, ev)
        for b in w:
            ev = self.lastw.get(b)
            if ev is not None and not (ev[0] == e):
                self._need(e, ev)
            rd = self.readers.get(b)
            if rd:
                for sk, val in rd.items():
                    if sk == e:
                        continue
                    self._need(e, (sk, val))
        if dsem is not None:
            self.semkeys.add(dsem)
            self.cnt[dsem] = self.cnt.get(dsem, 0) + 16
            ev = (dsem, self.cnt[dsem])
            self.prog[e].append(("dma", fn, dsem))
        else:
            if inc is None:
                inc = (e != "pe")
            if inc:
                self.cnt[e] = self.cnt.get(e, 0) + 1
                ev = (e, self.cnt[e])
                self.prog[e].append(("opi", fn, e))
            else:
                ev = (e, self.cnt.get(e, 0) + 1)
                self.prog[e].append(("op", fn, None))
        for b in w:
            self.lastw[b] = ev
            self.readers[b] = {}
        for b in r:
            d = self.readers.setdefault(b, {})
            if d.get(ev[0], 0) < ev[1]:
                d[ev[0]] = ev[1]

    def replay(self, e, eng):
        semh = self.semh
        for it in self.prog[e]:
            k = it[0]
            if k == "wait":
                eng.wait_ge(semh[it[1]], it[2])
            elif k == "dma":
                it[1](eng).then_inc(semh[it[2]], 16)
            elif k == "opi":
                it[1](eng).then_inc(semh[it[2]], 1)
            else:
                it[1](eng)


def build_program(nseq, layers, prep=True, dbg=False):
    nc = bass.Bass("TRN2", target_bir_lowering=False)
    NL = len(layers)
    dr = lambda name, shape, dt, kind: nc.dram_tensor(name, shape, dt, kind=kind).ap()
    x_d = dr("x", [nseq, SEQ, D], F32, "ExternalInput")
    meta_d = dr("meta", [NMETA, D], F32, "ExternalInput")
    par_d = dr("params", [DEPTH, 128, NPC], F32, "ExternalInput")
    cf_d = dr("cf32", [128, NCF], F32, "ExternalInput")
    cb_d = dr("cbf", [128, NCB], F32, "ExternalInput")
    wsrc = {
        "in": dr("w_in", [DEPTH, D, DIN], F32, "ExternalInput"),
        "sd": dr("w_sd", [DEPTH, 2048, D], F32, "ExternalInput"),
        "ad": dr("w_ad", [DEPTH, D, D], F32, "ExternalInput"),
        "o": dr("w_o", [DEPTH, D, D], F32, "ExternalInput"),
        "up": dr("w_up", [DEPTH, D, 4096], F32, "ExternalInput"),
        "down": dr("w_down", [DEPTH, 4096, D], F32, "ExternalInput"),
    }
    wbf = {k: dr("wb_" + k, list(v.shape), BF16, "Internal") for k, v in wsrc.items()}
    out_d = dr("out", [nseq, SEQ, D], F32, "ExternalOutput")
    smeta_d = dr("smeta", [len(layers), 128, 2048], F32, "Internal")

    T = Trk()
    sb = {}
    dbgs = {}
    import contextlib
    es = contextlib.ExitStack()

    def SB(name, shape, dt):
        t = es.enter_context(nc.sbuf_tensor(name, shape, dt))
        sb[name] = t
        return t

    hT = SB("hT", [128, 8, TT], F32)
    xnT = SB("xnT", [128, 8, TT], BF16)
    big = SB("big", [128, 32 * TT], BF16)
    yT = SB("yT", [128, 16, TT], BF16)
    scr = SB("scr", [128, 31 * 1024], mybir.dt.uint8)
    fz = SB("fz", [128, 16], F32)
    dtT = SB("dtT", [128, 4, 32], F32)
    adtT = SB("adtT", [128, 4, 32], F32)
    Sst = [SB("S%d" % i, [128, 2048], F32) for i in range(NL)]
    tails = [SB("tail%d" % i, [128, 32, 3], F32) for i in range(NL)]
    tails_m = [SB("tailm%d" % i, [128, 32, 3], F32) for i in range(NL)]
    kprev_t = [SB("kpt%d" % i, [128, 4, 128], BF16) for i in range(NL)]
    kprev_b = [SB("kpb%d" % i, [128, 4, 128], BF16) for i in range(NL)]
    kmeta_t = [SB("kmt%d" % i, [128, 4, 16], BF16) for i in range(NL)]
    kmeta_b = [SB("kmb%d" % i, [128, 4, 16], BF16) for i in range(NL)]
    vprev = [SB("vp%d" % i, [128, 256], BF16) for i in range(NL)]
    vmeta = [SB("vm%d" % i, [16, 256], BF16) for i in range(NL)]
    NRING = 4
    ring = [SB("ring%d" % i, [128, 4096], BF16) for i in range(NRING)]
    cf = SB("cf", [128, NCF], F32)
    cb = SB("cb", [128, NCB], BF16)
    par = SB("par", [128, DEPTH, NPC], F32)
    abc = SB("abc", [128, DEPTH, 32], F32)
    esk = SB("esk", [128, DEPTH, 8], F32)
    ps = [es.enter_context(nc.psum_tensor("ps%d" % i, [128, 512], F32)) for i in range(8)]

    tri_f, strict_f, ones_f, ident_f = (cf[:, i * 128:(i + 1) * 128] for i in range(4))
    tri_b, strict_b, ones_b, ident_b, bones_b, prot_b = (cb[:, i * 128:(i + 1) * 128] for i in range(6))
    cosT = cb[:, 768:768 + NPOS]
    sinT = cb[:, 768 + NPOS:768 + 2 * NPOS]

    def P(l, name):
        o, w = PC[name]
        return par[:, l, o:o + w]

    xsT = big[:, 0:16 * TT].rearrange("p (c t) -> p c t", c=16)
    BT = big[:, 16 * TT:24 * TT].rearrange("p (c t) -> p c t", c=8)
    CT = big[:, 24 * TT:32 * TT].rearrange("p (c t) -> p c t", c=8)
    qT = big[:, 0:8 * TT].rearrange("p (c t) -> p c t", c=8)
    kc_t = big[:, 8 * TT:12 * TT].rearrange("p (c t) -> p c t", c=4)
    kc_b = big[:, 12 * TT:16 * TT].rearrange("p (c t) -> p c t", c=4)
    Vt = big[:, 16 * TT:18 * TT].rearrange("p (b c) -> p b c", b=4)
    yaT = big[:, 18 * TT:26 * TT].rearrange("p (c t) -> p c t", c=8)
    hid = big[:, :].rearrange("p (c t) -> p c t", c=32)

    class Carve:
        def __init__(self):
            self.o = 0

        def get(self, nbytes, dt, shape=None):
            a = scr[:, self.o:self.o + nbytes].bitcast(dt)
            self.o += nbytes
            assert self.o <= 31 * 1024, self.o
            return a

    c = Carve()
    sq = c.get(8 * TT * 2, BF16).rearrange("p (c t) -> p c t", c=8)
    rstd = c.get(TT * 4, F32)
    ubuf = [c.get(520 * 4, F32) for _ in range(2)]
    accb = [c.get(TT * 4, F32) for _ in range(2)]
    stage = [c.get(1024 * 4, F32) for _ in range(2)]
    t3b = [c.get(TT * 4, F32) for _ in range(2)]
    c = Carve()
    xdt = c.get(4096, BF16)
    xdtd = c.get(4096, BF16)
    Btm = c.get(2048, BF16)
    CBm = c.get(2048, BF16).rearrange("p (g l) -> p g l", g=8)
    rseg2 = [c.get(2048, F32).rearrange("p (r l) -> p r l", r=4) for _ in range(2)]
    Lt2 = [c.get(1024, BF16).rearrange("p (r l) -> p r l", r=4) for _ in range(2)]
    Ebc2 = [c.get(1024, BF16).rearrange("p (r l) -> p r l", r=4) for _ in range(2)]
    Mt2 = [c.get(1024, BF16).rearrange("p (r l) -> p r l", r=4) for _ in range(2)]
    Ce2 = [c.get(1024, BF16).rearrange("p (r l) -> p r l", r=4) for _ in range(2)]
    S_bf = c.get(4096, BF16)
    dsc = c.get(128, F32)
    etot = c.get(128, F32)
    c = Carve()
    zs = [c.get(TT * 2, BF16) for _ in range(2)]
    sqy = [c.get(TT * 2, BF16) for _ in range(4)]
    rg2 = [c.get(TT * 4, F32) for _ in range(2)]
    c = Carve()
    qf2 = [c.get(TT * 4, F32) for _ in range(2)]
    qsq2 = [c.get(TT * 2, BF16) for _ in range(2)]
    qr2 = [c.get(TT * 4, F32) for _ in range(2)]
    qn2 = [c.get(TT * 2, BF16) for _ in range(2)]
    t12 = [c.get(TT * 4, F32) for _ in range(2)]
    t22 = [c.get(TT * 4, F32) for _ in range(2)]
    Pown2 = [c.get(1024, BF16) for _ in range(2)]
    Pprev2 = [c.get(1024, BF16) for _ in range(2)]
    Pmeta2 = [c.get(1024, BF16) for _ in range(2)]
    den2 = [c.get(1024, F32) for _ in range(2)]
    c = Carve()
    mT = c.get(8 * TT * 2, BF16).rearrange("p (c t) -> p c t", c=8)
    gs_sb = c.get(4 * TT * 2, BF16).rearrange("p (c t) -> p c t", c=4)
    ga_sb = c.get(4 * TT * 2, BF16).rearrange("p (c t) -> p c t", c=4)
    m1 = c.get(TT * 4, F32)
    m2 = c.get(TT * 4, F32)
    c = Carve()
    _ = c.get(8 * TT * 2 + TT * 4, BF16)
    rl = [c.get(TT * 4, F32) for _ in range(2)]

    ALLSCR = ([("sq", i) for i in range(8)] + ["rstd"] + [(n, i) for n in ("u", "acc", "stage", "xdt", "xdtd", "CBm", "zs",
              "rl", "rg") for i in range(2)] + [("sqy", i) for i in range(4)] + ["Btm", "S_bf", "dsc", "etot", "m1", "m2"] +
              [(n, i) for n in ("qf", "qsq", "qr", "qn", "t1", "t2", "rseg", "Lt", "Ebc", "Mt", "Ce", "den", "t3", ("P", "own"), ("P", "prev"), ("P", "meta"))
               for i in range(2)] +
              [("mT", i) for i in range(8)] + [("gs", i) for i in range(4)] + [("ga", i) for i in range(4)])

    BIGA = [("xsT", j) for j in range(16)] + [("BT", j) for j in range(8)] + [("CT", j) for j in range(8)]
    BIGB = [("qT", j) for j in range(8)] + ["kc_t", "kc_b"] + [("Vt", j) for j in range(4)] + \
           [("yaT", j) for j in range(8)]
    BIGC = [("hid", j) for j in range(32)]

    def mm(out, lhsT, rhs, start, stop, r, w, inc=None, tp=None):
        if inc is None:
            inc = stop
        if tp is None:
            T.op("pe", lambda e: e.matmul(out, lhsT=lhsT, rhs=rhs, start=start, stop=stop), r=r, w=w, inc=inc)
        else:
            T.op("pe", lambda e: e.matmul(out, lhsT=lhsT, rhs=rhs, start=start, stop=stop, tile_position=tp),
                 r=r, w=w, inc=inc)

    def tr(out, in_, ident, r, w, inc=False):
        T.op("pe", lambda e: e.transpose(out, in_, ident), r=r, w=w, inc=inc)

    def act(out, in_, func, r, w, bias=None, scale=None):
        kw = {}
        if bias is not None:
            kw["bias"] = bias
        if scale is not None:
            kw["scale"] = scale
        T.op("act", lambda e: e.activation(out=out, in_=in_, func=func, **kw), r=r, w=w)

    def tt(eng, out, in0, in1, op, r, w):
        T.op(eng, lambda e: e.tensor_tensor(out=out, in0=in0, in1=in1, op=op), r=r, w=w)

    def ts(eng, out, in0, s1, op0, r, w, s2=None, op1=None):
        if op1 is None:
            T.op(eng, lambda e: e.tensor_scalar(out=out, in0=in0, scalar1=s1, scalar2=None, op0=op0), r=r, w=w)
        else:
            T.op(eng, lambda e: e.tensor_scalar(out=out, in0=in0, scalar1=s1, scalar2=s2, op0=op0, op1=op1),
                 r=r, w=w)

    def stt(eng, out, in0, scalar, in1, op0, op1, r, w):
        T.op(eng, lambda e: e.scalar_tensor_tensor(out=out, in0=in0, scalar=scalar, in1=in1, op0=op0, op1=op1),
             r=r, w=w)

    def cp(eng, out, in_, r, w):
        if eng == "act":
            act(out, in_, AF.Copy, r, w)
        else:
            T.op(eng, lambda e: e.tensor_copy(out=out, in_=in_), r=r, w=w)

    def dma(out, in_, r, w, dsem, eng="sp"):
        T.op(eng, lambda e: e.dma_start(out=out, in_=in_), r=r, w=w, dsem=dsem)

    def dump(name, ap, rkeys):
        if not dbg or name in dbgs:
            return
        shp = list(ap.shape)
        d = nc.dram_tensor("dbg_" + name, shp, F32, kind="ExternalOutput").ap()
        dbgs[name] = d
        dma(d, ap, r=list(rkeys), w=[("dbg", name)], dsem="d_dbg", eng="pool")

    dma(cf[:, :], cf_d[:, :], r=[], w=["cf"], dsem="d_const")
    dma(par[:, :, :], par_d.rearrange("l p c -> p l c"), r=[], w=["par"], dsem="d_const2")
    dma(cb[:, :], cb_d[:, :], r=[], w=["cb"], dsem="d_cb", eng="pool")
    CONST = ["cf", "cb", "par", "derived"]
    for li in range(DEPTH):
        o, w_ = PC["alog"]
        act(abc[:, li, :], par[:, li, o:o + w_], AF.Exp, r=["par"], w=["derived"])
        o, w_ = PC["sink"]
        act(esk[:, li, :], par[:, li, o:o + w_], AF.Exp, r=["par"], w=["derived"])
    ts("dve", abc[:, :, :], abc[:, :, :], -1.0, ALU.mult, r=["derived"], w=["derived"])
    for i in range(NL):
        for nm, buf in (("kpt", kprev_t), ("kpb", kprev_b), ("kmt", kmeta_t), ("kmb", kmeta_b)):
            T.op("pool", lambda e, b=buf[i]: e.memset(b[:, :, :], 0.0), r=[], w=[(nm, i)])

    prep_sem = {}
    if prep:
        for li_, l in enumerate(layers):
            key = "d_prep%d" % li_
            for name in ("in", "sd", "ad", "o", "up", "down"):
                src, dst = wsrc[name], wbf[name]
                rows = src.shape[1]
                for r0 in range(0, rows, 128):
                    dma(dst[l, r0:r0 + 128, :], src[l, r0:r0 + 128, :], r=[], w=[("wbf", l)], dsem=key, eng="pool")

    def wsched(l):
        s = []
        s += [("in", l, OX + 512 * i, 512) for i in range(8)]
        s += [("in", l, ODT, 32)]
        s += [("in", l, OZ + 512 * i, 512) for i in range(4)]
        s += [("in", l, OQ + 512 * i, 512) for i in range(2)]
        s += [("in", l, OKK, 256), ("in", l, OV, 256)]
        for h in range(2):
            s += [("in", l, OG + 512 * h, 512), ("in", l, OG + 1024 + 512 * h, 512), ("ad", l, h), ("sd", l, 2 * h),
                  ("sd", l, 2 * h + 1)]
        s += [("o", l, h) for h in range(2)]
        s += [("up", l, i) for i in range(8)]
        s += [("down", l, i) for i in range(8)]
        return s

    tiles = []
    for s_ in range(nseq):
        if s_ == 0:
            tiles.append((s_, -1))
        for t_ in range(4):
            tiles.append((s_, t_))
    gsched = []
    for (s_, t_) in tiles:
        for l in layers:
            gsched += wsched(l)
    wstate = {"issued": 0, "next": 0}

    def w_issue(k):
        d = gsched[k]
        slot = k % NRING
        rb = ring[slot]
        name, l = d[0], d[1]
        if name == "in":
            c0, ncol = d[2], d[3]
            dst = rb[:, 0:8 * ncol].rearrange("p (k f) -> p k f", k=8)
            src = wbf["in"][l].rearrange("(k p) f -> p k f", p=128)[:, :, c0:c0 + ncol]
        elif name == "sd":
            dst = rb[:, :].rearrange("p (k f) -> p k f", k=16)
            src = wbf["sd"][l].rearrange("(k p) f -> p k f", p=128)[:, :, d[2] * 256:(d[2] + 1) * 256]
        elif name in ("ad", "o"):
            dst = rb[:, :].rearrange("p (k f) -> p k f", k=8)
            src = wbf[name][l].rearrange("(k p) f -> p k f", p=128)[:, :, d[2] * 512:(d[2] + 1) * 512]
        elif name == "up":
            dst = rb[:, :].rearrange("p (k f) -> p k f", k=8)
            src = wbf["up"][l].rearrange("(k p) f -> p k f", p=128)[:, :, d[2] * 512:(d[2] + 1) * 512]
        else:
            dst = rb[:, :].rearrange("p (k f) -> p k f", k=32)
            src = wbf["down"][l].rearrange("(k p) f -> p k f", p=128)[:, :, d[2] * 128:(d[2] + 1) * 128]
        dma(dst, src, r=[("wbf", l)], w=[("ring", slot)], dsem="d_ring%d" % slot)

    held = set()
    released = set()

    def wget(expect, hold=False):
        k = wstate["next"]
        assert gsched[k] == expect, (gsched[k], expect)
        for m in range(max(0, k - NRING - 1), k):
            if m not in held:
                released.add(m)
        if hold:
            held.add(k)
        while wstate["issued"] < min(len(gsched), k + NRING):
            m = wstate["issued"]
            if m >= NRING and (m - NRING) not in released:
                break
            w_issue(m)
            wstate["issued"] += 1
        assert wstate["issued"] > k, ("weight ring overflow", k, expect)
        wstate["next"] = k + 1
        slot = k % NRING
        return ring[slot], ("ring", slot)

    def wunhold():
        held.clear()

    psrot = {"i": 0}
    DENSE_BANKS = [0, 1, 4, 5, 6, 7]

    def dense_ps():
        k = psrot["i"]
        psrot["i"] = (k + 1) % len(DENSE_BANKS)
        i = DENSE_BANKS[k]
        return ps[i], ("ps", i)

    def rsq(out, in_ps, n, rk, wk_):
        act(out, in_ps, AF.Ln, r=rk, w=[wk_], bias=EPS, scale=1.0 / n)
        act(out, out, AF.Exp, r=[wk_], w=[wk_], scale=-0.5)

    def rmsnorm_tile(l, gname, Tn):
        g = P(l, gname)
        for kc in range(8):
            act(sq[:, kc, :Tn], hT[:, kc, :Tn], AF.Square, r=["hT"], w=[("sq", kc)])
        pt, pk = dense_ps()
        for kc in range(8):
            mm(pt[:, :Tn], ones_b, sq[:, kc, :Tn], kc == 0, kc == 7, r=[("sq", kc), "cb"], w=[pk])
        rsq(rstd[:, :Tn], pt[:, :Tn], D, [pk], "rstd")
        for kc in range(8):
            stt("dve", xnT[:, kc, :Tn], hT[:, kc, :Tn], g[:, kc:kc + 1], rstd[:, :Tn], ALU.mult, ALU.mult,
                r=["hT", "rstd", "par"], w=[("xnT", kc)])

    XN = [("xnT", kc) for kc in range(8)]

    def proj_chunk(wt, wk, col, Tn, m0=0, m1_=128, out_ps=None, tp=None, okey=None):
        wv = wt
        if out_ps is None:
            pt, pk = dense_ps()
            o = pt[:, :Tn]
        else:
            o, pk = out_ps, okey
        for kc in range(8):
            mm(o, wv[:, kc, col + m0:col + m1_], xnT[:, kc, :Tn], kc == 0, kc == 7, r=[wk] + XN, w=[pk], tp=tp)
        return o, pk

    def layer(li, l, Tn, BLK, NB, is_meta, tile_t, pos0):
        first_real = (tile_t == 0)
        fence(ALLSCR + BIGC + BIGA)
        rmsnorm_tile(l, "g1", Tn)
        dd = (tile_t == 0 and li == 0)
        if dd:
            dump("xnT", xnT[:, :, :], XN)
        cw = P(l, "cw")
        cbias = P(l, "cb")
        pend = []
        for i in range(8):
            wt, wk = wget(("in", l, OX + 512 * i, 512))
            wv = wt[:, :].rearrange("p (k f) -> p k f", k=8)
            for jj in range(4):
                j = 4 * i + jj
                o, pk = proj_chunk(wv, wk, jj * 128, Tn)
                u = ubuf[j % 2]
                uk = ("u", j % 2)
                if is_meta:
                    T.op("pool", lambda e, u=u: e.memset(u[:, 0:3], 0.0), r=[], w=[uk])
                else:
                    cp("pool", u[:, 0:3], tails[li][:, j, :], r=[("tail", li)], w=[uk])
                act(u[:, 3:3 + Tn], o, AF.Copy, r=[pk], w=[uk])
                cp("pool", tails[li][:, j, :], u[:, Tn:Tn + 3], r=[uk], w=[("tail", li)])
                a = accb[j % 2]
                ak = ("acc", j % 2)
                t3 = t3b[j % 2]
                t3k = ("t3", j % 2)
                act(t3[:, :Tn], o, AF.Copy, r=[pk, "par"], w=[t3k], scale=cw[:, 4 * j + 3:4 * j + 4])
                stt("dve", a[:, :Tn], u[:, 0:Tn], cw[:, 4 * j:4 * j + 1], t3[:, :Tn], ALU.mult, ALU.add,
                    r=[uk, t3k, "par"], w=[ak])
                for tap in range(1, 3):
                    stt("dve", a[:, :Tn], u[:, tap:tap + Tn], cw[:, 4 * j + tap:4 * j + tap + 1], a[:, :Tn],
                        ALU.mult, ALU.add, r=[uk, ak, "par"], w=[ak])
                if j < 16:
                    dst, dk = xsT[:, j, :Tn], ("xsT", j)
                elif j < 24:
                    dst, dk = BT[:, j - 16, :Tn], ("BT", j - 16)
                else:
                    dst, dk = CT[:, j - 24, :Tn], ("CT", j - 24)
                pend.append((dst, a, ak, dk, j))
                if len(pend) > 1:
                    d_, a_, ak_, dk_, j_ = pend.pop(0)
                    act(d_, a_[:, :Tn], AF.Silu, r=[ak_, "par"], w=[dk_], bias=cbias[:, j_:j_ + 1])
        while pend:
            d_, a_, ak_, dk_, j_ = pend.pop(0)
            act(d_, a_[:, :Tn], AF.Silu, r=[ak_, "par"], w=[dk_], bias=cbias[:, j_:j_ + 1])
        if is_meta:
            cp("pool", tails_m[li][:, :, :], tails[li][:, :, :], r=[("tail", li)], w=[("tailm", li)])
        wt, wk = wget(("in", l, ODT, 32))
        wv = wt[:, 0:256].rearrange("p (k f) -> p k f", k=8)
        for b in range(NB):
            pk = ("ps", 3)
            for kc in range(8):
                mm(ps[3][0:BLK, 0:32], xnT[:, kc, b * BLK:(b + 1) * BLK], wv[:, kc, :], kc == 0, kc == 7,
                   r=[wk] + XN, w=[pk])
            tt("dve", dtT[0:BLK, b, :], ps[3][0:BLK, 0:32], P(l, "dtb")[0:BLK, :], ALU.add, r=[pk, "par"],
               w=[("dt", b)])
            act(dtT[0:BLK, b, :], dtT[0:BLK, b, :], AF.Exp, r=[("dt", b)], w=[("dt", b)])
            act(dtT[0:BLK, b, :], dtT[0:BLK, b, :], AF.Ln, r=[("dt", b)], w=[("dt", b)], bias=1.0)
            tt("dve", adtT[0:BLK, b, :], dtT[0:BLK, b, :], abc[0:BLK, l, :], ALU.mult, r=[("dt", b), "derived"],
               w=[("adt", b)])
        if dd:
            dump("xsT", xsT[:, :, :], BIGA)
            dump("BT", BT[:, :, :], BIGA)
            dump("CT", CT[:, :, :], BIGA)
            dump("dtT", dtT[:, :, :], [("dt", b) for b in range(4)])
        fence(ALLSCR)
        S = Sst[li]
        SK = ("S", li)
        for b in range(NB):
            c0 = b * BLK
            blk = slice(c0, c0 + BLK)
            has_state = not is_meta
            for half in range(2):
                pk = ("ps", 2)
                pbf = ps[2][:, :].bitcast(BF16)
                for jj in range(8):
                    j = half * 8 + jj
                    tr(pbf[0:BLK, jj * 128:(jj + 1) * 128], xsT[:, j, blk], ident_b, r=[("xsT", j), "cb"], w=[pk],
                       inc=(jj == 7))
                tt("dve", xdt[0:BLK, half * 1024:(half + 1) * 1024].rearrange("p (h q) -> p h q", h=16),
                   pbf[0:BLK, 0:1024].rearrange("p (h q) -> p h q", h=16),
                   dtT[0:BLK, b, half * 16:(half + 1) * 16].unsqueeze(2).to_broadcast([BLK, 16, 64]), ALU.mult,
                   r=[pk, ("dt", b)], w=[("xdt", half)])
            pk = ("ps", 2)
            pbf = ps[2][:, :].bitcast(BF16)
            for g in range(8):
                tr(pbf[0:BLK, g * 128:(g + 1) * 128], BT[:, g, blk], ident_b, r=[("BT", g), "cb"], w=[pk],
                   inc=(g == 7))
            act(Btm[0:BLK, :], pbf[0:BLK, 0:1024], AF.Copy, r=[pk], w=["Btm"])
            pk = ("ps", 3)
            mm(ps[3][0:BLK, 0:32], strict_f[0:BLK, 0:BLK], adtT[0:BLK, b, :], True, True, r=[("adt", b), "cf"],
               w=[pk], inc=False)
            mm(ps[3][:, 32:64], ones_f[0:BLK, :], adtT[0:BLK, b, :], True, True, r=[("adt", b), "cf"], w=[pk],
               inc=True)
            act(dsc[0:BLK, :], ps[3][0:BLK, 0:32], AF.Exp, r=[pk], w=["dsc"])
            act(etot[:, :], ps[3][:, 32:64], AF.Exp, r=[pk], w=["etot"])
            for half in range(2):
                tt("pool", xdtd[0:BLK, half * 1024:(half + 1) * 1024].rearrange("p (h q) -> p h q", h=16),
                   xdt[0:BLK, half * 1024:(half + 1) * 1024].rearrange("p (h q) -> p h q", h=16),
                   dsc[0:BLK, half * 16:(half + 1) * 16].unsqueeze(2).to_broadcast([BLK, 16, 64]), ALU.mult,
                   r=[("xdt", half), "dsc"], w=[("xdtd", half)])
            for half in range(2):
                pk = ("ps", 4)
                for gg in range(4):
                    g = half * 4 + gg
                    mm(ps[4][0:BLK, gg * 128:gg * 128 + BLK], BT[:, g, blk], CT[:, g, blk], True, True,
                       r=[("BT", g), ("CT", g)], w=[pk], inc=(gg == 3))
                tt("dve", CBm[0:BLK, half * 4:half * 4 + 4, 0:BLK],
                   ps[4][0:BLK, :].rearrange("p (g l) -> p g l", g=4)[:, :, 0:BLK],
                   tri_f[0:BLK, 0:BLK].unsqueeze(1).to_broadcast([BLK, 4, BLK]), ALU.mult, r=[pk, "cf"],
                   w=[("CBm", half)])
            if has_state:
                cp("act", S_bf[:, :], S[:, :], r=[SK], w=["S_bf"])
            for q4 in range(4):
                pk = ("ps", 5)
                for gg in range(2):
                    g = q4 * 2 + gg
                    mm(ps[5][:, gg * 256:(gg + 1) * 256], Btm[0:BLK, g * 128:(g + 1) * 128],
                       xdtd[0:BLK, g * 256:(g + 1) * 256], True, True, r=["Btm", ("xdtd", g // 4)], w=[pk],
                       inc=(gg == 1))
                sl = slice(q4 * 512, (q4 + 1) * 512)
                if has_state:
                    tt("dve", S[:, sl].rearrange("p (h q) -> p h q", h=8), S[:, sl].rearrange("p (h q) -> p h q", h=8),
                       etot[:, q4 * 8:(q4 + 1) * 8].unsqueeze(2).to_broadcast([128, 8, 64]), ALU.mult,
                       r=[SK, "etot", "S_bf"], w=[SK])
                    tt("dve", S[:, sl], S[:, sl], ps[5][:, :], ALU.add, r=[SK, pk], w=[SK])
                else:
                    cp("dve", S[:, sl], ps[5][:, :], r=[pk], w=[SK])
            def G1(g):
                gp = g % 2
                rseg, Lt, Ebc, Mt, Ce = rseg2[gp], Lt2[gp], Ebc2[gp], Mt2[gp], Ce2[gp]
                krs, klt, keb, kmt, kce = ("rseg", gp), ("Lt", gp), ("Ebc", gp), ("Mt", gp), ("Ce", gp)
                i6, i7 = (6, 7) if gp == 0 else (4, 3)
                tt("pool", rseg[0:BLK, :, 0:BLK], tri_f[0:BLK, 0:BLK].unsqueeze(1).to_broadcast([BLK, 4, BLK]),
                   adtT[0:BLK, b, 4 * g:4 * g + 4].unsqueeze(2).to_broadcast([BLK, 4, BLK]), ALU.mult,
                   r=[("adt", b), "cf"], w=[krs])
                pk6, pk7 = ("ps", i6), ("ps", i7)
                for r_ in range(4):
                    mm(ps[i6][0:BLK, r_ * 128:r_ * 128 + BLK], strict_f[0:BLK, 0:BLK], rseg[0:BLK, r_, 0:BLK], True,
                       True, r=[krs, "cf"], w=[pk6], inc=(r_ == 3))
                if has_state:
                    for r_ in range(4):
                        mm(ps[i7][:, r_ * 128:r_ * 128 + BLK], ones_f[0:BLK, :], rseg[0:BLK, r_, 0:BLK], True, True,
                           r=[krs, "cf"], w=[pk7], inc=(r_ == 3))
                act(Lt[0:BLK, :, 0:BLK], ps[i6][0:BLK, :].rearrange("p (r l) -> p r l", r=4)[:, :, 0:BLK], AF.Exp,
                    r=[pk6], w=[klt])
                tt("dve", Mt[0:BLK, :, 0:BLK], Lt[0:BLK, :, 0:BLK],
                   CBm[0:BLK, g, 0:BLK].unsqueeze(1).to_broadcast([BLK, 4, BLK]), ALU.mult,
                   r=[klt, ("CBm", g // 4)], w=[kmt])
                if has_state:
                    act(Ebc[:, :, 0:BLK], ps[i7][:, :].rearrange("p (r l) -> p r l", r=4)[:, :, 0:BLK], AF.Exp,
                        r=[pk7], w=[keb])
                    tt("dve", Ce[:, :, 0:BLK], Ebc[:, :, 0:BLK],
                       CT[:, g, blk].unsqueeze(1).to_broadcast([128, 4, BLK]), ALU.mult, r=[keb, ("CT", g)],
                       w=[kce])

            def G2(g):
                gp = g % 2
                Mt, Ce = Mt2[gp], Ce2[gp]
                kmt, kce = ("Mt", gp), ("Ce", gp)
                pi = g % 2
                pyk = ("ps", pi)
                for r_ in range(4):
                    h = 4 * g + r_
                    half = r_ % 2
                    jj = r_ // 2
                    o = ps[pi][half * 64:(half + 1) * 64, jj * 128:jj * 128 + BLK]
                    tp = (0, 64) if half else None
                    mm(o, xdt[0:BLK, h * 64:(h + 1) * 64], Mt[0:BLK, r_, 0:BLK], True, not has_state,
                       r=[("xdt", h // 16), kmt], w=[pyk], inc=(r_ == 3 and not has_state), tp=tp)
                    if has_state:
                        mm(o, S_bf[:, h * 64:(h + 1) * 64], Ce[:, r_, 0:BLK], False, True, r=["S_bf", kce], w=[pyk],
                           inc=(r_ == 3), tp=tp)
                for jj in range(2):
                    j = 2 * g + jj
                    stt("dve", yT[:, j, blk], xsT[:, j, blk], P(l, "dsk")[:, j:j + 1],
                        ps[pi][:, jj * 128:jj * 128 + BLK], ALU.mult, ALU.add, r=[("xsT", j), "par", pyk],
                        w=[("yT", j)])

            G1(0)
            for g in range(8):
                if g + 1 < 8:
                    G1(g + 1)
                G2(g)
        if dd:
            dump("y0", yT[:, :, :], [("yT", j) for j in range(16)])
            dump("S", S[:, :], [SK])
        if is_meta:
            dma(smeta_d[li], S[:, :], r=[SK], w=[("smeta", li)], dsem="d_smo%d" % li)
        fence(ALLSCR)
        def zB2(jq):
            for pp in range(2):
                j0 = 4 * jq + 2 * pp
                pi_ = 2 + pp
                mm(ps[pi_][:, :Tn], ones_b, sqy[2 * pp][:, :Tn], True, False, r=[("sqy", 2 * pp), "cb"],
                   w=[("ps", pi_)])
                mm(ps[pi_][:, :Tn], ones_b, sqy[2 * pp + 1][:, :Tn], False, True, r=[("sqy", 2 * pp + 1), "cb"],
                   w=[("ps", pi_)])
            for pp in range(2):
                act(rg2[pp][:, :Tn], ps[2 + pp][:, :Tn], AF.Ln, r=[("ps", 2 + pp)], w=[("rg", pp)], bias=EPS,
                    scale=1.0 / 256)
            for pp in range(2):
                act(rg2[pp][:, :Tn], rg2[pp][:, :Tn], AF.Exp, r=[("rg", pp)], w=[("rg", pp)], scale=-0.5)
            for pp in range(2):
                j0 = 4 * jq + 2 * pp
                for j2 in (j0, j0 + 1):
                    stt("dve", yT[:, j2, :Tn], yT[:, j2, :Tn], P(l, "ng")[:, j2:j2 + 1], rg2[pp][:, :Tn], ALU.mult,
                        ALU.mult, r=[("yT", j2), ("rg", pp), "par"], w=[("yT", j2)])

        for i in range(4):
            wt, wk = wget(("in", l, OZ + 512 * i, 512))
            wv = wt[:, :].rearrange("p (k f) -> p k f", k=8)
            for jj in range(4):
                j = 4 * i + jj
                o, pk = proj_chunk(wv, wk, jj * 128, Tn)
                z_ = zs[j % 2]
                act(z_[:, :Tn], o, AF.Silu, r=[pk], w=[("zs", j % 2)])
                tt("pool", yT[:, j, :Tn], yT[:, j, :Tn], z_[:, :Tn], ALU.mult, r=[("yT", j), ("zs", j % 2)],
                   w=[("yT", j)])
                tt("dve", sqy[jj][:, :Tn], yT[:, j, :Tn], yT[:, j, :Tn], ALU.mult, r=[("yT", j)], w=[("sqy", jj)])
            zB2(i)
        if dd:
            dump("yn", yT[:, :, :], [("yT", j) for j in range(16)])
        fence(ALLSCR + BIGA + BIGB)
        if not is_meta:
            T.op("pool", lambda e: e.memset(kc_t[64:128, :, :], 0.0), r=[], w=["kc_t"])
            T.op("pool", lambda e: e.memset(kc_b[0:64, :, :], 0.0), r=[], w=["kc_b"])
        posl = slice(pos0, pos0 + Tn)

        nrc = {"i": 0}
        nr_pend = []

        def nr_s1(o, pk, gcol, dsts):
            i = nrc["i"] % 2
            nrc["i"] += 1
            qf, qsq, qr = qf2[i], qsq2[i], qr2[i]
            act(qf[:, :Tn], o, AF.Copy, r=[pk], w=[("qf", i)])
            tt("dve", qsq[:, :Tn], qf[:, :Tn], qf[:, :Tn], ALU.mult, r=[("qf", i)], w=[("qsq", i)])
            mm(ps[2][:, :Tn], bones_b, qsq[:, :Tn], True, True, r=[("qsq", i), "cb"], w=[("ps", 2)])
            rsq(qr[:, :Tn], ps[2][:, :Tn], 64, [("ps", 2)], ("qr", i))
            nr_pend.append((i, gcol, dsts))

        def nr_s2():
            i, gcol, dsts = nr_pend.pop(0)
            qf, qr, qn, t1, t2 = qf2[i], qr2[i], qn2[i], t12[i], t22[i]
            stt("dve", qn[:, :Tn], qf[:, :Tn], gcol, qr[:, :Tn], ALU.mult, ALU.mult,
                r=[("qf", i), ("qr", i), "par"], w=[("qn", i)])
            mm(ps[3][:, :Tn], prot_b, qn[:, :Tn], True, True, r=[("qn", i), "cb"], w=[("ps", 3)])
            tt("pool", t1[:, :Tn], qn[:, :Tn], cosT[:, posl], ALU.mult, r=[("qn", i), "cb"], w=[("t1", i)])
            tt("dve", t2[:, :Tn], ps[3][:, :Tn], sinT[:, posl], ALU.mult, r=[("ps", 3), "cb"], w=[("t2", i)])
            for (psl, dst, dk) in dsts:
                tt("pool", dst, t1[psl, :Tn], t2[psl, :Tn], ALU.add, r=[("t1", i), ("t2", i)], w=[dk])

        def norm_rope(o, pk, gcol, dsts):
            nr_s1(o, pk, gcol, dsts)
            if len(nr_pend) > 1:
                nr_s2()

        for i in range(2):
            wt, wk = wget(("in", l, OQ + 512 * i, 512))
            wv = wt[:, :].rearrange("p (k f) -> p k f", k=8)
            for jj in range(4):
                cq = 4 * i + jj
                o, pk = proj_chunk(wv, wk, jj * 128, Tn)
                norm_rope(o, pk, P(l, "qg")[:, 0:1], [(slice(0, 128), qT[:, cq, :Tn], ("qT", cq))])
        wt, wk = wget(("in", l, OKK, 256))
        wv = wt[:, 0:2048].rearrange("p (k f) -> p k f", k=8)
        for j in range(4):
            pt, pk = dense_ps()
            for kc in range(8):
                mm(pt[0:64, :Tn], wv[:, kc, j * 64:(j + 1) * 64], xnT[:, kc, :Tn], kc == 0, kc == 7, r=[wk] + XN,
                   w=[pk], inc=False)
            for kc in range(8):
                mm(pt[64:128, :Tn], wv[:, kc, j * 64:(j + 1) * 64], xnT[:, kc, :Tn], kc == 0, kc == 7, r=[wk] + XN,
                   w=[pk], inc=(kc == 7), tp=(0, 64))
            if is_meta:
                d_t, d_b = kmeta_t[li][0:64, j, 0:Tn], kmeta_b[li][64:128, j, 0:Tn]
                kt_, kb_ = ("kmt", li), ("kmb", li)
            else:
                d_t, d_b = kc_t[0:64, j, :Tn], kc_b[64:128, j, :Tn]
                kt_, kb_ = "kc_t", "kc_b"
            norm_rope(pt[:, :Tn], pk, P(l, "kg")[:, 0:1], [(slice(0, 64), d_t, kt_), (slice(64, 128), d_b, kb_)])
        while nr_pend:
            nr_s2()
        wt, wk = wget(("in", l, OV, 256))
        wv = wt[:, 0:2048].rearrange("p (k f) -> p k f", k=8)
        for b in range(NB):
            pt, pk = dense_ps()
            for kc in range(8):
                mm(pt[0:BLK, 0:256], xnT[:, kc, b * BLK:(b + 1) * BLK], wv[:, kc, :], kc == 0, kc == 7,
                   r=[wk] + XN, w=[pk])
            if is_meta:
                act(vmeta[li][0:BLK, :], pt[0:BLK, 0:256], AF.Copy, r=[pk], w=[("vm", li)])
            else:
                act(Vt[0:BLK, b, :], pt[0:BLK, 0:256], AF.Copy, r=[pk], w=[("Vt", b)])
        if dd:
            dump("qT", qT[:, :, :], BIGB)
            dump("kc_t", kc_t[:, :, :], BIGB)
            dump("kc_b", kc_b[:, :, :], BIGB)
            dump("Vt", Vt[:, :, :], BIGB)
        QK = [("qT", cq) for cq in range(8)]
        def cq_(ap2d):
            return ap2d.rearrange("p (c q) -> p c q", c=2)[:, :, 0:BLK]

        def A1(b, j):
            blk = slice(b * BLK, (b + 1) * BLK)
            jp_ = j % 2
            Pown, Pprev, Pmeta, den = Pown2[jp_], Pprev2[jp_], Pmeta2[jp_], den2[jp_]
            io, ipv, ime, i7 = (4, 5, 6, 7) if jp_ == 0 else (1, 2, 3, 0)
            parts = []
            if is_meta:
                parts.append(("own", io, BLK, kmeta_t[li][:, j, 0:BLK], kmeta_b[li][:, j, 0:BLK],
                              vmeta[li][0:BLK, j * 64:(j + 1) * 64], [("kmt", li), ("kmb", li), ("vm", li)], Pown,
                              tri_b))
            else:
                parts.append(("own", io, BLK, kc_t[:, j, blk], kc_b[:, j, blk], Vt[0:BLK, b, j * 64:(j + 1) * 64],
                              ["kc_t", "kc_b", ("Vt", b)], Pown, tri_b))
                if b > 0:
                    pb = slice((b - 1) * BLK, b * BLK)
                    parts.append(("prev", ipv, BLK, kc_t[:, j, pb], kc_b[:, j, pb],
                                  Vt[0:BLK, b - 1, j * 64:(j + 1) * 64], ["kc_t", "kc_b", ("Vt", b - 1)], Pprev,
                                  strict_b))
                elif not first_real:
                    parts.append(("prev", ipv, 128, kprev_t[li][:, j, :], kprev_b[li][:, j, :],
                                  vprev[li][:, j * 64:(j + 1) * 64], [("kpt", li), ("kpb", li), ("vp", li)], Pprev,
                                  strict_b))
                parts.append(("meta", ime, 16, kmeta_t[li][:, j, :], kmeta_b[li][:, j, :],
                              vmeta[li][0:16, j * 64:(j + 1) * 64], [("kmt", li), ("kmb", li), ("vm", li)], Pmeta,
                              None))
            rhs_q = qT[:, 2 * j:2 * j + 2, blk]
            for (nm, pi, KR, lt_, lb_, vap, keys, Pb, mask) in parts:
                pk = ("ps", pi)
                mm(cq_(ps[pi][0:KR, 0:256]), lt_, rhs_q, True, True, r=keys + QK, w=[pk], inc=False)
                mm(cq_(ps[pi][0:KR, 256:512]), lb_, rhs_q, True, True, r=keys + QK, w=[pk], inc=True)
                pv = ps[pi][0:KR, :].rearrange("p (a q) -> p a q", a=4)[:, :, 0:BLK]
                Pm = Pb[0:KR, :].rearrange("p (a q) -> p a q", a=4)[:, :, 0:BLK]
                act(Pm, pv, AF.Exp, r=[pk], w=[(("P", nm), jp_)], scale=0.125)
                if mask is not None:
                    tt("pool", Pm, Pm, mask[0:KR, 0:BLK].unsqueeze(1).to_broadcast([KR, 4, BLK]), ALU.mult,
                       r=[(("P", nm), jp_), "cb"], w=[(("P", nm), jp_)])
            return (b, j, blk, jp_, parts, den, i7)

        def A2(ctx):
            (b, j, blk, jp_, parts, den, i7) = ctx
            pk7 = ("ps", i7)
            npart = len(parts)
            for s in range(2):
                tp = (0, 64) if s else None
                for ip, (nm, pi, KR, lt_, lb_, vap, keys, Pb, mask) in enumerate(parts):
                    rhs = cq_(Pb[0:KR, s * 256:(s + 1) * 256])
                    mm(cq_(ps[i7][s * 64:(s + 1) * 64, 0:256]), vap, rhs, ip == 0, ip == npart - 1,
                       r=keys + [(("P", nm), jp_)], w=[pk7], inc=False, tp=tp)
                for ip, (nm, pi, KR, lt_, lb_, vap, keys, Pb, mask) in enumerate(parts):
                    rhs = cq_(Pb[0:KR, s * 256:(s + 1) * 256])
                    mm(cq_(ps[i7][s * 64:(s + 1) * 64, 256:512]), ones_b[0:KR, 0:64], rhs, ip == 0,
                       ip == npart - 1, r=[(("P", nm), jp_), "cb"], w=[pk7], inc=(s == 1 and ip == npart - 1), tp=tp)
            dv = cq_(den[:, 0:256])
            tt("dve", dv, cq_(ps[i7][:, 256:512]),
               esk[:, l, 2 * j:2 * j + 2].unsqueeze(2).to_broadcast([128, 2, BLK]), ALU.add,
               r=[pk7, "derived"], w=[("den", jp_)])
            T.op("dve", lambda e, dv=dv: e.reciprocal(out=dv, in_=dv), r=[("den", jp_)], w=[("den", jp_)])
            tt("dve", yaT[:, 2 * j:2 * j + 2, blk], cq_(ps[i7][:, 0:256]), dv,
               ALU.mult, r=[pk7, ("den", jp_)], w=[("yaT", 2 * j), ("yaT", 2 * j + 1)])

        items = [(b, j) for b in range(NB) for j in range(4)]
        ctxs = []
        for it in items:
            ctxs.append(A1(*it))
            if len(ctxs) > 1:
                A2(ctxs.pop(0))
        while ctxs:
            A2(ctxs.pop(0))
        if not is_meta:
            cp("pool", kprev_t[li][0:64, :, :], kc_t[0:64, :, Tn - 128:Tn], r=["kc_t"], w=[("kpt", li)])
            cp("pool", kprev_b[li][64:128, :, :], kc_b[64:128, :, Tn - 128:Tn], r=["kc_b"], w=[("kpb", li)])
            cp("pool", vprev[li][:, :], Vt[:, NB - 1, :], r=[("Vt", NB - 1)], w=[("vp", li)])
        if dd:
            dump("yaT", yaT[:, :, :], BIGB)
        fence(ALLSCR)
        YN = [("yT", j) for j in range(16)]
        YA = [("yaT", j) for j in range(8)]
        for h in range(2):
            wgs, kgs = wget(("in", l, OG + 512 * h, 512))
            for jj in range(4):
                oc = 4 * h + jj
                o, pk = proj_chunk(wgs[:, :].rearrange("p (k f) -> p k f", k=8), kgs, jj * 128, Tn)
                act(gs_sb[:, jj, :Tn], o, AF.Sigmoid, r=[pk, "par"], w=[("gs", jj)], bias=P(l, "bg")[:, oc:oc + 1])
            wga, kga = wget(("in", l, OG + 1024 + 512 * h, 512))
            for jj in range(4):
                oc = 4 * h + jj
                o, pk = proj_chunk(wga[:, :].rearrange("p (k f) -> p k f", k=8), kga, jj * 128, Tn)
                act(ga_sb[:, jj, :Tn], o, AF.Sigmoid, r=[pk, "par"], w=[("ga", jj)],
                    bias=P(l, "bg")[:, 8 + oc:9 + oc])
            wad, kad = wget(("ad", l, h), hold=True)
            wadv = wad[:, :].rearrange("p (k f) -> p k f", k=8)
            for jp in range(2):
                wsd, ksd = wget(("sd", l, 2 * h + jp))
                wsdv = wsd[:, :].rearrange("p (k f) -> p k f", k=16)
                for j2 in range(2):
                    jj = 2 * jp + j2
                    oc = 4 * h + jj
                    pa, pka = ps[2], ("ps", 2)
                    for kc in range(16):
                        mm(pa[:, :Tn], wsdv[:, kc, j2 * 128:(j2 + 1) * 128], yT[:, kc, :Tn], kc == 0, kc == 15,
                           r=[ksd] + YN, w=[pka])
                    pb_, pkb = ps[3], ("ps", 3)
                    for kc in range(8):
                        mm(pb_[:, :Tn], wadv[:, kc, jj * 128:(jj + 1) * 128], yaT[:, kc, :Tn], kc == 0, kc == 7,
                           r=[kad] + YA, w=[pkb])
                    tt("dve", m1[:, :Tn], pa[:, :Tn], gs_sb[:, jj, :Tn], ALU.mult, r=[pka, ("gs", jj)], w=["m1"])
                    tt("dve", m2[:, :Tn], pb_[:, :Tn], ga_sb[:, jj, :Tn], ALU.mult, r=[pkb, ("ga", jj)], w=["m2"])
                    tt("pool", mT[:, oc, :Tn], m1[:, :Tn], m2[:, :Tn], ALU.add, r=["m1", "m2"], w=[("mT", oc)])
            wunhold()
        if dd:
            dump("mT", mT[:, :, :], [("mT", j) for j in range(8)])
        MT = [("mT", j) for j in range(8)]
        for h in range(2):
            wt, wk = wget(("o", l, h))
            wv = wt[:, :].rearrange("p (k f) -> p k f", k=8)
            for jj in range(4):
                oc = 4 * h + jj
                pt, pk = dense_ps()
                for kc in range(8):
                    mm(pt[:, :Tn], wv[:, kc, jj * 128:(jj + 1) * 128], mT[:, kc, :Tn], kc == 0, kc == 7, r=[wk] + MT,
                       w=[pk])
                tt("dve", hT[:, oc, :Tn], hT[:, oc, :Tn], pt[:, :Tn], ALU.add, r=["hT", pk], w=["hT"])
        if dd:
            dump("h1", hT[:, :, :], ["hT"])
        fence(ALLSCR + BIGB + BIGC)
        rmsnorm_tile(l, "g2", Tn)
        for i in range(8):
            wt, wk = wget(("up", l, i))
            wv = wt[:, :].rearrange("p (k f) -> p k f", k=8)
            for jj in range(4):
                j = 4 * i + jj
                o, pk = proj_chunk(wv, wk, jj * 128, Tn)
                r_ = rl[j % 2]
                act(r_[:, :Tn], o, AF.Relu, r=[pk], w=[("rl", j % 2)])
                tt("pool", hid[:, j, :Tn], r_[:, :Tn], r_[:, :Tn], ALU.mult, r=[("rl", j % 2)], w=[("hid", j)])
        HID = [("hid", j) for j in range(32)]
        for oc in range(8):
            wt, wk = wget(("down", l, oc))
            wv = wt[:, :].rearrange("p (k f) -> p k f", k=32)
            pt, pk = dense_ps()
            for kc in range(32):
                mm(pt[:, :Tn], wv[:, kc, :], hid[:, kc, :Tn], kc == 0, kc == 31, r=[wk] + HID, w=[pk])
            tt("dve", hT[:, oc, :Tn], hT[:, oc, :Tn], pt[:, :Tn], ALU.add, r=["hT", pk], w=["hT"])


    def fence(keys):
        T.op("pool", lambda e: e.memset(fz[0:1, 0:1], 0.0), r=[], w=list(keys))

    for (s_, t_) in tiles:
        is_meta = (t_ < 0)
        if is_meta:
            Tn, BLK, NB, pos0 = NMETA, NMETA, 1, 0
        else:
            Tn, BLK, NB, pos0 = TT, 128, 4, NMETA + TT * t_
        if s_ > 0 and t_ == 0:
            for li in range(NL):
                dma(Sst[li][:, :], smeta_d[li], r=[("smeta", li)], w=[("S", li)], dsem="d_smi%d" % li)
                cp("pool", tails[li][:, :, :], tails_m[li][:, :, :], r=[("tailm", li)], w=[("tail", li)])
        fence(ALLSCR)
        for b in range(NB):
            st = stage[b % 2]
            sk = ("stage", b % 2)
            if is_meta:
                src = meta_d[:, :]
            else:
                src = x_d[s_, t_ * TT + b * 128:t_ * TT + (b + 1) * 128, :]
            dma(st[0:BLK, :], src, r=[], w=[sk], dsem="d_stage%d" % (b % 2))
            for half in range(2):
                pi = 2 + half
                for kk in range(4):
                    kc = half * 4 + kk
                    tr(ps[pi][:, kk * 128:kk * 128 + BLK], st[0:BLK, kc * 128:(kc + 1) * 128], ident_f[0:BLK, 0:BLK],
                       r=[sk, "cf"], w=[("ps", pi)], inc=(kk == 3))
                cp("act" if half else "dve", hT[:, half * 4:half * 4 + 4, b * BLK:(b + 1) * BLK],
                   ps[pi][:, :].rearrange("p (k t) -> p k t", k=4)[:, :, 0:BLK], r=[("ps", pi)], w=["hT"])
        for li, l in enumerate(layers):
            layer(li, l, Tn, BLK, NB, is_meta, t_, pos0)
        fence(ALLSCR)
        if not is_meta:
            for b in range(NB):
                st = stage[b % 2]
                sk = ("stage", b % 2)
                for half in range(2):
                    pi = 2 + half
                    for kk in range(4):
                        kc = half * 4 + kk
                        tr(ps[pi][:, kk * 128:(kk + 1) * 128], hT[:, kc, b * 128:(b + 1) * 128], ident_f, r=["hT", "cf"],
                           w=[("ps", pi)], inc=(kk == 3))
                    cp("act" if half else "dve", st[:, half * 512:(half + 1) * 512], ps[pi][:, :], r=[("ps", pi)],
                       w=[sk])
                dma(out_d[s_, t_ * TT + b * 128:t_ * TT + (b + 1) * 128, :], st[:, :], r=[sk], w=[("out", s_, t_, b)],
                    dsem="d_out%d" % (b % 2))
    for k in ("d_out0", "d_out1", "d_dbg"):
        if k in T.cnt:
            T.prog["sp"].append(("wait", k, T.cnt[k]))

    for k in sorted(T.semkeys):
        T.semh[k] = es.enter_context(nc.semaphore("s_" + k))
    with nc.Block() as block:
        @block.tensor
        def _(e):
            T.replay("pe", e)

        @block.scalar
        def _(e):
            T.replay("act", e)

        @block.vector
        def _(e):
            T.replay("dve", e)

        @block.gpsimd
        def _(e):
            T.replay("pool", e)

        @block.sync
        def _(e):
            T.replay("sp", e)
    es.close()
    nc._dbg_names = list(dbgs.keys())
    return nc, T


def host_consts():
    k = np.arange(128)
    tri = (k[:, None] <= k[None, :]).astype(np.float32)
    strict = (k[:, None] > k[None, :]).astype(np.float32)
    ones = np.ones((128, 128), np.float32)
    ident = np.eye(128, dtype=np.float32)
    bones = np.zeros((128, 128), np.float32)
    bones[:64, :64] = 1
    bones[64:, 64:] = 1
    prot = np.zeros((128, 128), np.float32)
    for blk in (0, 64):
        for d in range(32):
            prot[blk + d + 32, blk + d] = -1.0
            prot[blk + d, blk + d + 32] = 1.0
    half = 32
    inv_freq = (np.float32(10000.0) ** (-np.arange(half, dtype=np.float32) / np.float32(half))).astype(np.float32)
    pos = np.arange(NPOS).astype(np.float32)
    ang = (pos[:, None] * inv_freq[None, :]).astype(np.float32)
    cos = np.cos(ang).astype(np.float32)
    sin = np.sin(ang).astype(np.float32)
    p = np.arange(128) % 32
    cosT = cos[:, p].T
    sinT = sin[:, p].T
    cf = np.concatenate([tri, strict, ones, ident], axis=1)
    cbf = np.concatenate([tri, strict, ones, ident, bones, prot, cosT, sinT], axis=1)
    return np.ascontiguousarray(cf, np.float32), np.ascontiguousarray(cbf, np.float32)


def host_params(inp):
    par = np.zeros((DEPTH, 128, NPC), np.float32)

    def put(name, arr):
        o, w = PC[name]
        par[:, :, o:o + w] = arr

    fm = lambda a, n: a.reshape(DEPTH, n, 128).transpose(0, 2, 1)
    put("g1", fm(inp["norm1_g"], 8))
    put("g2", fm(inp["norm2_g"], 8))
    cw = inp["conv_w"].reshape(DEPTH, 4, 32, 128).transpose(0, 3, 2, 1).reshape(DEPTH, 128, 128)
    put("cw", cw)
    put("cb", fm(inp["conv_b"], 32))
    put("bg", fm(inp["b_gate"], 16))
    put("ng", fm(inp["ssd_norm_g"], 16))
    put("dsk", fm(np.repeat(inp["d_skip"], 64, axis=1), 16))
    put("qg", np.tile(inp["q_norm_g"], (1, 2))[:, :, None])
    put("kg", np.tile(inp["k_norm_g"], (1, 2))[:, :, None])
    sk = inp["sinks"].reshape(DEPTH, 8, 2)
    put("sink", np.repeat(sk.transpose(0, 2, 1), 64, axis=1))
    put("dtb", np.broadcast_to(inp["dt_bias"][:, None, :], (DEPTH, 128, 32)))
    put("alog", np.broadcast_to(inp["a_log"][:, None, :], (DEPTH, 128, 32)))
    return par


_PROG = {}


def run(inputs, nseq_per_core, ncores, layers, trace=False, dbg=False):
    inp = {k: np.asarray(v) for k, v in inputs.items()}
    key = (nseq_per_core, tuple(layers))
    if key not in _PROG:
        _PROG[key] = build_program(nseq_per_core, layers, dbg=dbg)
    nc, _ = _PROG[key]
    cf, cbf = host_consts()
    par = host_params(inp)
    x = np.ascontiguousarray(inp["x"], np.float32)
    shared = {
        "meta": np.ascontiguousarray(inp["meta_tokens"], np.float32), "params": par, "cf32": cf, "cbf": cbf,
        "w_in": np.ascontiguousarray(inp["w_in"], np.float32), "w_sd": np.ascontiguousarray(inp["w_ssd_down"], np.float32),
        "w_ad": np.ascontiguousarray(inp["w_attn_down"], np.float32), "w_o": np.ascontiguousarray(inp["w_o"], np.float32),
        "w_up": np.ascontiguousarray(inp["w_mlp_up"], np.float32),
        "w_down": np.ascontiguousarray(inp["w_mlp_down"], np.float32),
    }
    in_maps = []
    for c_ in range(ncores):
        m = dict(shared)
        m["x"] = x[c_ * nseq_per_core:(c_ + 1) * nseq_per_core]
        in_maps.append(m)
    res = run_bass_kernel_spmd(nc, in_maps, core_ids=list(range(ncores)), trace=trace)
    out = np.concatenate([np.asarray(r["out"]) for r in res.results], axis=0)
    return out.astype(np.float32), res


def kernel(**inputs):
    out, _ = run(inputs, 32 // NCORES, NCORES, list(range(DEPTH)))
    return out
```

```python
import numpy as np
import concourse.bass as bass
import concourse.mybir as mybir
from concourse.bass_utils import run_bass_kernel_spmd

F32 = mybir.dt.float32
BF16 = mybir.dt.bfloat16
ALU = mybir.AluOpType
AF = mybir.ActivationFunctionType

D = 1024
DIN = 9760
NMETA = 16
SEQ = 2048
DEPTH = 4
EPS = 1e-6
OZ, OX, ODT, OQ, OKK, OV, OG = 0, 2048, 6144, 6176, 7200, 7456, 7712
TT = 512
NCORES = 8

PC = {}
_o = 0
for _n, _w in [("g1", 8), ("g2", 8), ("cw", 128), ("cb", 32), ("bg", 16), ("ng", 16), ("dsk", 16),
               ("qg", 1), ("kg", 1), ("sink", 8), ("dtb", 32), ("alog", 32)]:
    PC[_n] = (_o, _w)
    _o += _w
NPC = _o
NCF = 4 * 128
NPOS = NMETA + SEQ
NCB = 6 * 128 + 2 * NPOS


class Trk:
    def __init__(self):
        self.cnt = {}
        self.waited = {e: {} for e in ("pe", "act", "dve", "pool", "sp")}
        self.lastw = {}
        self.readers = {}
        self.prog = {e: [] for e in ("pe", "act", "dve", "pool", "sp")}
        self.semkeys = set(["pe", "act", "dve", "pool"])
        self.semh = {}
        self.n = 0

    def _need(self, e, ev):
        semkey, val = ev
        if self.waited[e].get(semkey, 0) >= val:
            return
        self.waited[e][semkey] = val
        self.prog[e].append(("wait", semkey, val))

    def op(self, e, fn, r=(), w=(), inc=None, dsem=None):
        self.n += 1
        for b in r:
            ev = self.lastw.get(b)
            if ev is not None:
                if ev[0] == e and e == "pe":
                    continue
                self._need(e, ev)
        for b in w:
            ev = self.lastw.get(b)
            if ev is not None and not (ev[0] == e):
                self._need(e, ev)
            rd = self.readers.get(b)
            if rd:
                for sk, val in rd.items():
                    if sk == e:
                        continue
                    self._need(e, (sk, val))
        if dsem is not None:
            self.semkeys.add(dsem)
            self.cnt[dsem] = self.cnt.get(dsem, 0) + 16
            ev = (dsem, self.cnt[dsem])
            self.prog[e].append(("dma", fn, dsem))
        else:
            if inc is None:
                inc = (e != "pe")
            if inc:
                self.cnt[e] = self.cnt.get(e, 0) + 1
                ev = (e, self.cnt[e])
                self.prog[e].append(("opi", fn, e))
            else:
                ev = (e, self.cnt.get(e, 0) + 1)
                self.prog[e].append(("op", fn, None))
        for b in w:
            self.lastw[b] = ev
            self.readers[b] = {}
        for b in r:
            d = self.readers.setdefault(b, {})
            if d.get(ev[0], 0) < ev[1]:
                d[ev[0]] = ev[1]

    def replay(self, e, eng):
        semh = self.semh
        for it in self.prog[e]:
            k = it[0]
            if k == "wait":
                eng.wait_ge(semh[it[1]], it[2])
            elif k == "dma":
                it[1](eng).then_inc(semh[it[2]], 16)
            elif k == "opi":
                it[1](eng).then_inc(semh[it[2]], 1)
            else:
                it[1](eng)


def build_program(nseq, layers, prep=True, dbg=False):
    nc = bass.Bass("TRN2", target_bir_lowering=False)
    NL = len(layers)
    dr = lambda name, shape, dt, kind: nc.dram_tensor(name, shape, dt, kind=kind).ap()
    x_d = dr("x", [nseq, SEQ, D], F32, "ExternalInput")
    meta_d = dr("meta", [NMETA, D], F32, "ExternalInput")
    par_d = dr("params", [DEPTH, 128, NPC], F32, "ExternalInput")
    cf_d = dr("cf32", [128, NCF], F32, "ExternalInput")
    cb_d = dr("cbf", [128, NCB], F32, "ExternalInput")
    wsrc = {
        "in": dr("w_in", [DEPTH, D, DIN], F32, "ExternalInput"),
        "sd": dr("w_sd", [DEPTH, 2048, D], F32, "ExternalInput"),
        "ad": dr("w_ad", [DEPTH, D, D], F32, "ExternalInput"),
        "o": dr("w_o", [DEPTH, D, D], F32, "ExternalInput"),
        "up": dr("w_up", [DEPTH, D, 4096], F32, "ExternalInput"),
        "down": dr("w_down", [DEPTH, 4096, D], F32, "ExternalInput"),
    }
    wbf = {k: dr("wb_" + k, list(v.shape), BF16, "Internal") for k, v in wsrc.items()}
    out_d = dr("out", [nseq, SEQ, D], F32, "ExternalOutput")
    smeta_d = dr("smeta", [len(layers), 128, 2048], F32, "Internal")

    T = Trk()
    sb = {}
    dbgs = {}
    import contextlib
    es = contextlib.ExitStack()

    def SB(name, shape, dt):
        t = es.enter_context(nc.sbuf_tensor(name, shape, dt))
        sb[name] = t
        return t

    hT = SB("hT", [128, 8, TT], F32)
    xnT = SB("xnT", [128, 8, TT], BF16)
    big = SB("big", [128, 32 * TT], BF16)
    yT = SB("yT", [128, 16, TT], BF16)
    scr = SB("scr", [128, 31 * 1024], mybir.dt.uint8)
    fz = SB("fz", [128, 16], F32)
    dtT = SB("dtT", [128, 4, 32], F32)
    adtT = SB("adtT", [128, 4, 32], F32)
    Sst = [SB("S%d" % i, [128, 2048], F32) for i in range(NL)]
    tails = [SB("tail%d" % i, [128, 32, 3], F32) for i in range(NL)]
    tails_m = [SB("tailm%d" % i, [128, 32, 3], F32) for i in range(NL)]
    kprev_t = [SB("kpt%d" % i, [128, 4, 128], BF16) for i in range(NL)]
    kprev_b = [SB("kpb%d" % i, [128, 4, 128], BF16) for i in range(NL)]
    kmeta_t = [SB("kmt%d" % i, [128, 4, 16], BF16) for i in range(NL)]
    kmeta_b = [SB("kmb%d" % i, [128, 4, 16], BF16) for i in range(NL)]
    vprev = [SB("vp%d" % i, [128, 256], BF16) for i in range(NL)]
    vmeta = [SB("vm%d" % i, [16, 256], BF16) for i in range(NL)]
    NRING = 4
    ring = [SB("ring%d" % i, [128, 4096], BF16) for i in range(NRING)]
    cf = SB("cf", [128, NCF], F32)
    cb = SB("cb", [128, NCB], BF16)
    par = SB("par", [128, DEPTH, NPC], F32)
    abc = SB("abc", [128, DEPTH, 32], F32)
    esk = SB("esk", [128, DEPTH, 8], F32)
    ps = [es.enter_context(nc.psum_tensor("ps%d" % i, [128, 512], F32)) for i in range(8)]

    tri_f, strict_f, ones_f, ident_f = (cf[:, i * 128:(i + 1) * 128] for i in range(4))
    tri_b, strict_b, ones_b, ident_b, bones_b, prot_b = (cb[:, i * 128:(i + 1) * 128] for i in range(6))
    cosT = cb[:, 768:768 + NPOS]
    sinT = cb[:, 768 + NPOS:768 + 2 * NPOS]

    def P(l, name):
        o, w = PC[name]
        return par[:, l, o:o + w]

    xsT = big[:, 0:16 * TT].rearrange("p (c t) -> p c t", c=16)
    BT = big[:, 16 * TT:24 * TT].rearrange("p (c t) -> p c t", c=8)
    CT = big[:, 24 * TT:32 * TT].rearrange("p (c t) -> p c t", c=8)
    qT = big[:, 0:8 * TT].rearrange("p (c t) -> p c t", c=8)
    kc_t = big[:, 8 * TT:12 * TT].rearrange("p (c t) -> p c t", c=4)
    kc_b = big[:, 12 * TT:16 * TT].rearrange("p (c t) -> p c t", c=4)
    Vt = big[:, 16 * TT:18 * TT].rearrange("p (b c) -> p b c", b=4)
    yaT = big[:, 18 * TT:26 * TT].rearrange("p (c t) -> p c t", c=8)
    hid = big[:, :].rearrange("p (c t) -> p c t", c=32)

    class Carve:
        def __init__(self):
            self.o = 0

        def get(self, nbytes, dt, shape=None):
            a = scr[:, self.o:self.o + nbytes].bitcast(dt)
            self.o += nbytes
            assert self.o <= 31 * 1024, self.o
            return a

    c = Carve()
    sq = c.get(8 * TT * 2, BF16).rearrange("p (c t) -> p c t", c=8)
    rstd = c.get(TT * 4, F32)
    ubuf = [c.get(520 * 4, F32) for _ in range(2)]
    accb = [c.get(TT * 4, F32) for _ in range(2)]
    stage = [c.get(1024 * 4, F32) for _ in range(2)]
    t3b = [c.get(TT * 4, F32) for _ in range(2)]
    c = Carve()
    xdt = c.get(4096, BF16)
    xdtd = c.get(4096, BF16)
    Btm = c.get(2048, BF16)
    CBm = c.get(2048, BF16).rearrange("p (g l) -> p g l", g=8)
    rseg2 = [c.get(2048, F32).rearrange("p (r l) -> p r l", r=4) for _ in range(2)]
    Lt2 = [c.get(1024, BF16).rearrange("p (r l) -> p r l", r=4) for _ in range(2)]
    Ebc2 = [c.get(1024, BF16).rearrange("p (r l) -> p r l", r=4) for _ in range(2)]
    Mt2 = [c.get(1024, BF16).rearrange("p (r l) -> p r l", r=4) for _ in range(2)]
    Ce2 = [c.get(1024, BF16).rearrange("p (r l) -> p r l", r=4) for _ in range(2)]
    S_bf = c.get(4096, BF16)
    dsc = c.get(128, F32)
    etot = c.get(128, F32)
    c = Carve()
    zs = [c.get(TT * 2, BF16) for _ in range(2)]
    sqy = [c.get(TT * 2, BF16) for _ in range(4)]
    rg2 = [c.get(TT * 4, F32) for _ in range(2)]
    c = Carve()
    qf2 = [c.get(TT * 4, F32) for _ in range(2)]
    qsq2 = [c.get(TT * 2, BF16) for _ in range(2)]
    qr2 = [c.get(TT * 4, F32) for _ in range(2)]
    qn2 = [c.get(TT * 2, BF16) for _ in range(2)]
    t12 = [c.get(TT * 4, F32) for _ in range(2)]
    t22 = [c.get(TT * 4, F32) for _ in range(2)]
    Pown2 = [c.get(1024, BF16) for _ in range(2)]
    Pprev2 = [c.get(1024, BF16) for _ in range(2)]
    Pmeta2 = [c.get(1024, BF16) for _ in range(2)]
    den2 = [c.get(1024, F32) for _ in range(2)]
    c = Carve()
    mT = c.get(8 * TT * 2, BF16).rearrange("p (c t) -> p c t", c=8)
    gs_sb = c.get(4 * TT * 2, BF16).rearrange("p (c t) -> p c t", c=4)
    ga_sb = c.get(4 * TT * 2, BF16).rearrange("p (c t) -> p c t", c=4)
    m1 = c.get(TT * 4, F32)
    m2 = c.get(TT * 4, F32)
    c = Carve()
    _ = c.get(8 * TT * 2 + TT * 4, BF16)
    rl = [c.get(TT * 4, F32) for _ in range(2)]

    ALLSCR = ([("sq", i) for i in range(8)] + ["rstd"] + [(n, i) for n in ("u", "acc", "stage", "xdt", "xdtd", "CBm", "zs",
              "rl", "rg") for i in range(2)] + [("sqy", i) for i in range(4)] + ["Btm", "S_bf", "dsc", "etot", "m1", "m2"] +
              [(n, i) for n in ("qf", "qsq", "qr", "qn", "t1", "t2", "rseg", "Lt", "Ebc", "Mt", "Ce", "den", "t3", ("P", "own"), ("P", "prev"), ("P", "meta"))
               for i in range(2)] +
              [("mT", i) for i in range(8)] + [("gs", i) for i in range(4)] + [("ga", i) for i in range(4)])

    BIGA = [("xsT", j) for j in range(16)] + [("BT", j) for j in range(8)] + [("CT", j) for j in range(8)]
    BIGB = [("qT", j) for j in range(8)] + ["kc_t", "kc_b"] + [("Vt", j) for j in range(4)] + \
           [("yaT", j) for j in range(8)]
    BIGC = [("hid", j) for j in range(32)]

    def mm(out, lhsT, rhs, start, stop, r, w, inc=None, tp=None):
        if inc is None:
            inc = stop
        if tp is None:
            T.op("pe", lambda e: e.matmul(out, lhsT=lhsT, rhs=rhs, start=start, stop=stop), r=r, w=w, inc=inc)
        else:
            T.op("pe", lambda e: e.matmul(out, lhsT=lhsT, rhs=rhs, start=start, stop=stop, tile_position=tp),
                 r=r, w=w, inc=inc)

    def tr(out, in_, ident, r, w, inc=False):
        T.op("pe", lambda e: e.transpose(out, in_, ident), r=r, w=w, inc=inc)

    def act(out, in_, func, r, w, bias=None, scale=None):
        kw = {}
        if bias is not None:
            kw["bias"] = bias
        if scale is not None:
            kw["scale"] = scale
        T.op("act", lambda e: e.activation(out=out, in_=in_, func=func, **kw), r=r, w=w)

    def tt(eng, out, in0, in1, op, r, w):
        T.op(eng, lambda e: e.tensor_tensor(out=out, in0=in0, in1=in1, op=op), r=r, w=w)

    def ts(eng, out, in0, s1, op0, r, w, s2=None, op1=None):
        if op1 is None:
            T.op(eng, lambda e: e.tensor_scalar(out=out, in0=in0, scalar1=s1, scalar2=None, op0=op0), r=r, w=w)
        else:
            T.op(eng, lambda e: e.tensor_scalar(out=out, in0=in0, scalar1=s1, scalar2=s2, op0=op0, op1=op1),
                 r=r, w=w)

    def stt(eng, out, in0, scalar, in1, op0, op1, r, w):
        T.op(eng, lambda e: e.scalar_tensor_tensor(out=out, in0=in0, scalar=scalar, in1=in1, op0=op0, op1=op1),
             r=r, w=w)

    def cp(eng, out, in_, r, w):
        if eng == "act":
            act(out, in_, AF.Copy, r, w)
        else:
            T.op(eng, lambda e: e.tensor_copy(out=out, in_=in_), r=r, w=w)

    def dma(out, in_, r, w, dsem, eng="sp"):
        T.op(eng, lambda e: e.dma_start(out=out, in_=in_), r=r, w=w, dsem=dsem)

    def dump(name, ap, rkeys):
        if not dbg or name in dbgs:
            return
        shp = list(ap.shape)
        d = nc.dram_tensor("dbg_" + name, shp, F32, kind="ExternalOutput").ap()
        dbgs[name] = d
        dma(d, ap, r=list(rkeys), w=[("dbg", name)], dsem="d_dbg", eng="pool")

    dma(cf[:, :], cf_d[:, :], r=[], w=["cf"], dsem="d_const")
    dma(par[:, :, :], par_d.rearrange("l p c -> p l c"), r=[], w=["par"], dsem="d_const2")
    dma(cb[:, :], cb_d[:, :], r=[], w=["cb"], dsem="d_cb", eng="pool")
    CONST = ["cf", "cb", "par", "derived"]
    for li in range(DEPTH):
        o, w_ = PC["alog"]
        act(abc[:, li, :], par[:, li, o:o + w_], AF.Exp, r=["par"], w=["derived"])
        o, w_ = PC["sink"]
        act(esk[:, li, :], par[:, li, o:o + w_], AF.Exp, r=["par"], w=["derived"])
    ts("dve", abc[:, :, :], abc[:, :, :], -1.0, ALU.mult, r=["derived"], w=["derived"])
    for i in range(NL):
        for nm, buf in (("kpt", kprev_t), ("kpb", kprev_b), ("kmt", kmeta_t), ("kmb", kmeta_b)):
            T.op("pool", lambda e, b=buf[i]: e.memset(b[:, :, :], 0.0), r=[], w=[(nm, i)])

    prep_sem = {}
    if prep:
        for li_, l in enumerate(layers):
            key = "d_prep%d" % li_
            for name in ("in", "sd", "ad", "o", "up", "down"):
                src, dst = wsrc[name], wbf[name]
                rows = src.shape[1]
                for r0 in range(0, rows, 128):
                    dma(dst[l, r0:r0 + 128, :], src[l, r0:r0 + 128, :], r=[], w=[("wbf", l)], dsem=key, eng="pool")

    def wsched(l):
        s = []
        s += [("in", l, OX + 512 * i, 512) for i in range(8)]
        s += [("in", l, ODT, 32)]
        s += [("in", l, OZ + 512 * i, 512) for i in range(4)]
        s += [("in", l, OQ + 512 * i, 512) for i in range(2)]
        s += [("in", l, OKK, 256), ("in", l, OV, 256)]
        for h in range(2):
            s += [("in", l, OG + 512 * h, 512), ("in", l, OG + 1024 + 512 * h, 512), ("ad", l, h), ("sd", l, 2 * h),
                  ("sd", l, 2 * h + 1)]
        s += [("o", l, h) for h in range(2)]
        s += [("up", l, i) for i in range(8)]
        s += [("down", l, i) for i in range(8)]
        return s

    tiles = []
    for s_ in range(nseq):
        if s_ == 0:
            tiles.append((s_, -1))
        for t_ in range(4):
            tiles.append((s_, t_))
    gsched = []
    for (s_, t_) in tiles:
        for l in layers:
            gsched += wsched(l)
    wstate = {"issued": 0, "next": 0}

    def w_issue(k):
        d = gsched[k]
        slot = k % NRING
        rb = ring[slot]
        name, l = d[0], d[1]
        if name == "in":
            c0, ncol = d[2], d[3]
            dst = rb[:, 0:8 * ncol].rearrange("p (k f) -> p k f", k=8)
            src = wbf["in"][l].rearrange("(k p) f -> p k f", p=128)[:, :, c0:c0 + ncol]
        elif name == "sd":
            dst = rb[:, :].rearrange("p (k f) -> p k f", k=16)
            src = wbf["sd"][l].rearrange("(k p) f -> p k f", p=128)[:, :, d[2] * 256:(d[2] + 1) * 256]
        elif name in ("ad", "o"):
            dst = rb[:, :].rearrange("p (k f) -> p k f", k=8)
            src = wbf[name][l].rearrange("(k p) f -> p k f", p=128)[:, :, d[2] * 512:(d[2] + 1) * 512]
        elif name == "up":
            dst = rb[:, :].rearrange("p (k f) -> p k f", k=8)
            src = wbf["up"][l].rearrange("(k p) f -> p k f", p=128)[:, :, d[2] * 512:(d[2] + 1) * 512]
        else:
            dst = rb[:, :].rearrange("p (k f) -> p k f", k=32)
            src = wbf["down"][l].rearrange("(k p) f -> p k f", p=128)[:, :, d[2] * 128:(d[2] + 1) * 128]
        dma(dst, src, r=[("wbf", l)], w=[("ring", slot)], dsem="d_ring%d" % slot)

    held = set()
    released = set()

    def wget(expect, hold=False):
        k = wstate["next"]
        assert gsched[k] == expect, (gsched[k], expect)
        for m in range(max(0, k - NRING - 1), k):
            if m not in held:
                released.add(m)
        if hold:
            held.add(k)
        while wstate["issued"] < min(len(gsched), k + NRING):
            m = wstate["issued"]
            if m >= NRING and (m - NRING) not in released:
                break
            w_issue(m)
            wstate["issued"] += 1
        assert wstate["issued"] > k, ("weight ring overflow", k, expect)
        wstate["next"] = k + 1
        slot = k % NRING
        return ring[slot], ("ring", slot)

    def wunhold():
        held.clear()

    psrot = {"i": 0}
    DENSE_BANKS = [0, 1, 4, 5, 6, 7]

    def dense_ps():
        k = psrot["i"]
        psrot["i"] = (k + 1) % len(DENSE_BANKS)
        i = DENSE_BANKS[k]
        return ps[i], ("ps", i)

    def rsq(out, in_ps, n, rk, wk_):
        act(out, in_ps, AF.Ln, r=rk, w=[wk_], bias=EPS, scale=1.0 / n)
        act(out, out, AF.Exp, r=[wk_], w=[wk_], scale=-0.5)

    def rmsnorm_tile(l, gname, Tn):
        g = P(l, gname)
        for kc in range(8):
            act(sq[:, kc, :Tn], hT[:, kc, :Tn], AF.Square, r=["hT"], w=[("sq", kc)])
        pt, pk = dense_ps()
        for kc in range(8):
            mm(pt[:, :Tn], ones_b, sq[:, kc, :Tn], kc == 0, kc == 7, r=[("sq", kc), "cb"], w=[pk])
        rsq(rstd[:, :Tn], pt[:, :Tn], D, [pk], "rstd")
        for kc in range(8):
            stt("dve", xnT[:, kc, :Tn], hT[:, kc, :Tn], g[:, kc:kc + 1], rstd[:, :Tn], ALU.mult, ALU.mult,
                r=["hT", "rstd", "par"], w=[("xnT", kc)])

    XN = [("xnT", kc) for kc in range(8)]

    def proj_chunk(wt, wk, col, Tn, m0=0, m1_=128, out_ps=None, tp=None, okey=None):
        wv = wt
        if out_ps is None:
            pt, pk = dense_ps()
            o = pt[:, :Tn]
        else:
            o, pk = out_ps, okey
        for kc in range(8):
            mm(o, wv[:, kc, col + m0:col + m1_], xnT[:, kc, :Tn], kc == 0, kc == 7, r=[wk] + XN, w=[pk], tp=tp)
        return o, pk

    def layer(li, l, Tn, BLK, NB, is_meta, tile_t, pos0):
        first_real = (tile_t == 0)
        fence(ALLSCR + BIGC + BIGA)
        rmsnorm_tile(l, "g1", Tn)
        dd = (tile_t == 0 and li == 0)
        if dd:
            dump("xnT", xnT[:, :, :], XN)
        cw = P(l, "cw")
        cbias = P(l, "cb")
        pend = []
        for i in range(8):
            wt, wk = wget(("in", l, OX + 512 * i, 512))
            wv = wt[:, :].rearrange("p (k f) -> p k f", k=8)
            for jj in range(4):
                j = 4 * i + jj
                o, pk = proj_chunk(wv, wk, jj * 128, Tn)
                u = ubuf[j % 2]
                uk = ("u", j % 2)
                if is_meta:
                    T.op("pool", lambda e, u=u: e.memset(u[:, 0:3], 0.0), r=[], w=[uk])
                else:
                    cp("pool", u[:, 0:3], tails[li][:, j, :], r=[("tail", li)], w=[uk])
                act(u[:, 3:3 + Tn], o, AF.Copy, r=[pk], w=[uk])
                cp("pool", tails[li][:, j, :], u[:, Tn:Tn + 3], r=[uk], w=[("tail", li)])
                a = accb[j % 2]
                ak = ("acc", j % 2)
                t3 = t3b[j % 2]
                t3k = ("t3", j % 2)
                act(t3[:, :Tn], o, AF.Copy, r=[pk, "par"], w=[t3k], scale=cw[:, 4 * j + 3:4 * j + 4])
                stt("dve", a[:, :Tn], u[:, 0:Tn], cw[:, 4 * j:4 * j + 1], t3[:, :Tn], ALU.mult, ALU.add,
                    r=[uk, t3k, "par"], w=[ak])
                for tap in range(1, 3):
                    stt("dve", a[:, :Tn], u[:, tap:tap + Tn], cw[:, 4 * j + tap:4 * j + tap + 1], a[:, :Tn],
                        ALU.mult, ALU.add, r=[uk, ak, "par"], w=[ak])
                if j < 16:
                    dst, dk = xsT[:, j, :Tn], ("xsT", j)
                elif j < 24:
                    dst, dk = BT[:, j - 16, :Tn], ("BT", j - 16)
                else:
                    dst, dk = CT[:, j - 24, :Tn], ("CT", j - 24)
                pend.append((dst, a, ak, dk, j))
                if len(pend) > 1:
                    d_, a_, ak_, dk_, j_ = pend.pop(0)
                    act(d_, a_[:, :Tn], AF.Silu, r=[ak_, "par"], w=[dk_], bias=cbias[:, j_:j_ + 1])
        while pend:
            d_, a_, ak_, dk_, j_ = pend.pop(0)
            act(d_, a_[:, :Tn], AF.Silu, r=[ak_, "par"], w=[dk_], bias=cbias[:, j_:j_ + 1])
        if is_meta:
            cp("pool", tails_m[li][:, :, :], tails[li][:, :, :], r=[("tail", li)], w=[("tailm", li)])
        wt, wk = wget(("in", l, ODT, 32))
        wv = wt[:, 0:256].rearrange("p (k f) -> p k f", k=8)
        for b in range(NB):
            pk = ("ps", 3)
            for kc in range(8):
                mm(ps[3][0:BLK, 0:32], xnT[:, kc, b * BLK:(b + 1) * BLK], wv[:, kc, :], kc == 0, kc == 7,
                   r=[wk] + XN, w=[pk])
            tt("dve", dtT[0:BLK, b, :], ps[3][0:BLK, 0:32], P(l, "dtb")[0:BLK, :], ALU.add, r=[pk, "par"],
               w=[("dt", b)])
            act(dtT[0:BLK, b, :], dtT[0:BLK, b, :], AF.Exp, r=[("dt", b)], w=[("dt", b)])
            act(dtT[0:BLK, b, :], dtT[0:BLK, b, :], AF.Ln, r=[("dt", b)], w=[("dt", b)], bias=1.0)
            tt("dve", adtT[0:BLK, b, :], dtT[0:BLK, b, :], abc[0:BLK, l, :], ALU.mult, r=[("dt", b), "derived"],
               w=[("adt", b)])
        if dd:
            dump("xsT", xsT[:, :, :], BIGA)
            dump("BT", BT[:, :, :], BIGA)
            dump("CT", CT[:, :, :], BIGA)
            dump("dtT", dtT[:, :, :], [("dt", b) for b in range(4)])
        fence(ALLSCR)
        S = Sst[li]
        SK = ("S", li)
        for b in range(NB):
            c0 = b * BLK
            blk = slice(c0, c0 + BLK)
            has_state = not is_meta
            for half in range(2):
                pk = ("ps", 2)
                pbf = ps[2][:, :].bitcast(BF16)
                for jj in range(8):
                    j = half * 8 + jj
                    tr(pbf[0:BLK, jj * 128:(jj + 1) * 128], xsT[:, j, blk], ident_b, r=[("xsT", j), "cb"], w=[pk],
                       inc=(jj == 7))
                tt("dve", xdt[0:BLK, half * 1024:(half + 1) * 1024].rearrange("p (h q) -> p h q", h=16),
                   pbf[0:BLK, 0:1024].rearrange("p (h q) -> p h q", h=16),
                   dtT[0:BLK, b, half * 16:(half + 1) * 16].unsqueeze(2).to_broadcast([BLK, 16, 64]), ALU.mult,
                   r=[pk, ("dt", b)], w=[("xdt", half)])
            pk = ("ps", 2)
            pbf = ps[2][:, :].bitcast(BF16)
            for g in range(8):
                tr(pbf[0:BLK, g * 128:(g + 1) * 128], BT[:, g, blk], ident_b, r=[("BT", g), "cb"], w=[pk],
                   inc=(g == 7))
            act(Btm[0:BLK, :], pbf[0:BLK, 0:1024], AF.Copy, r=[pk], w=["Btm"])
            pk = ("ps", 3)
            mm(ps[3][0:BLK, 0:32], strict_f[0:BLK, 0:BLK], adtT[0:BLK, b, :], True, True, r=[("adt", b), "cf"],
               w=[pk], inc=False)
            mm(ps[3][:, 32:64], ones_f[0:BLK, :], adtT[0:BLK, b, :], True, True, r=[("adt", b), "cf"], w=[pk],
               inc=True)
            act(dsc[0:BLK, :], ps[3][0:BLK, 0:32], AF.Exp, r=[pk], w=["dsc"])
            act(etot[:, :], ps[3][:, 32:64], AF.Exp, r=[pk], w=["etot"])
            for half in range(2):
                tt("pool", xdtd[0:BLK, half * 1024:(half + 1) * 1024].rearrange("p (h q) -> p h q", h=16),
                   xdt[0:BLK, half * 1024:(half + 1) * 1024].rearrange("p (h q) -> p h q", h=16),
                   dsc[0:BLK, half * 16:(half + 1) * 16].unsqueeze(2).to_broadcast([BLK, 16, 64]), ALU.mult,
                   r=[("xdt", half), "dsc"], w=[("xdtd", half)])
            for half in range(2):
                pk = ("ps", 4)
                for gg in range(4):
                    g = half * 4 + gg
                    mm(ps[4][0:BLK, gg * 128:gg * 128 + BLK], BT[:, g, blk], CT[:, g, blk], True, True,
                       r=[("BT", g), ("CT", g)], w=[pk], inc=(gg == 3))
                tt("dve", CBm[0:BLK, half * 4:half * 4 + 4, 0:BLK],
                   ps[4][0:BLK, :].rearrange("p (g l) -> p g l", g=4)[:, :, 0:BLK],
                   tri_f[0:BLK, 0:BLK].unsqueeze(1).to_broadcast([BLK, 4, BLK]), ALU.mult, r=[pk, "cf"],
                   w=[("CBm", half)])
            if has_state:
                cp("act", S_bf[:, :], S[:, :], r=[SK], w=["S_bf"])
            for q4 in range(4):
                pk = ("ps", 5)
                for gg in range(2):
                    g = q4 * 2 + gg
                    mm(ps[5][:, gg * 256:(gg + 1) * 256], Btm[0:BLK, g * 128:(g + 1) * 128],
                       xdtd[0:BLK, g * 256:(g + 1) * 256], True, True, r=["Btm", ("xdtd", g // 4)], w=[pk],
                       inc=(gg == 1))
                sl = slice(q4 * 512, (q4 + 1) * 512)
                if has_state:
                    tt("dve", S[:, sl].rearrange("p (h q) -> p h q", h=8), S[:, sl].rearrange("p (h q) -> p h q", h=8),
                       etot[:, q4 * 8:(q4 + 1) * 8].unsqueeze(2).to_broadcast([128, 8, 64]), ALU.mult,
                       r=[SK, "etot", "S_bf"], w=[SK])
                    tt("dve", S[:, sl], S[:, sl], ps[5][:, :], ALU.add, r=[SK, pk], w=[SK])
                else:
                    cp("dve", S[:, sl], ps[5][:, :], r=[pk], w=[SK])
            def G1(g):
                gp = g % 2
                rseg, Lt, Ebc, Mt, Ce = rseg2[gp], Lt2[gp], Ebc2[gp], Mt2[gp], Ce2[gp]
                krs, klt, keb, kmt, kce = ("rseg", gp), ("Lt", gp), ("Ebc", gp), ("Mt", gp), ("Ce", gp)
                i6, i7 = (6, 7) if gp == 0 else (4, 3)
                tt("pool", rseg[0:BLK, :, 0:BLK], tri_f[0:BLK, 0:BLK].unsqueeze(1).to_broadcast([BLK, 4, BLK]),
                   adtT[0:BLK, b, 4 * g:4 * g + 4].unsqueeze(2).to_broadcast([BLK, 4, BLK]), ALU.mult,
                   r=[("adt", b), "cf"], w=[krs])
                pk6, pk7 = ("ps", i6), ("ps", i7)
                for r_ in range(4):
                    mm(ps[i6][0:BLK, r_ * 128:r_ * 128 + BLK], strict_f[0:BLK, 0:BLK], rseg[0:BLK, r_, 0:BLK], True,
                       True, r=[krs, "cf"], w=[pk6], inc=(r_ == 3))
                if has_state:
                    for r_ in range(4):
                        mm(ps[i7][:, r_ * 128:r_ * 128 + BLK], ones_f[0:BLK, :], rseg[0:BLK, r_, 0:BLK], True, True,
                           r=[krs, "cf"], w=[pk7], inc=(r_ == 3))
                act(Lt[0:BLK, :, 0:BLK], ps[i6][0:BLK, :].rearrange("p (r l) -> p r l", r=4)[:, :, 0:BLK], AF.Exp,
                    r=[pk6], w=[klt])
                tt("dve", Mt[0:BLK, :, 0:BLK], Lt[0:BLK, :, 0:BLK],
                   CBm[0:BLK, g, 0:BLK].unsqueeze(1).to_broadcast([BLK, 4, BLK]), ALU.mult,
                   r=[klt, ("CBm", g // 4)], w=[kmt])
                if has_state:
                    act(Ebc[:, :, 0:BLK], ps[i7][:, :].rearrange("p (r l) -> p r l", r=4)[:, :, 0:BLK], AF.Exp,
                        r=[pk7], w=[keb])
                    tt("dve", Ce[:, :, 0:BLK], Ebc[:, :, 0:BLK],
                       CT[:, g, blk].unsqueeze(1).to_broadcast([128, 4, BLK]), ALU.mult, r=[keb, ("CT", g)],
                       w=[kce])

            def G2(g):
                gp = g % 2
                Mt, Ce = Mt2[gp], Ce2[gp]
                kmt, kce = ("Mt", gp), ("Ce", gp)
                pi = g % 2
                pyk = ("ps", pi)
                for r_ in range(4):
                    h = 4 * g + r_
                    half = r_ % 2
                    jj = r_ // 2
                    o = ps[pi][half * 64:(half + 1) * 64, jj * 128:jj * 128 + BLK]
                    tp = (0, 64) if half else None
                    mm(o, xdt[0:BLK, h * 64:(h + 1) * 64], Mt[0:BLK, r_, 0:BLK], True, not has_state,
                       r=[("xdt", h // 16), kmt], w=[pyk], inc=(r_ == 3 and not has_state), tp=tp)
                    if has_state:
                        mm(o, S_bf[:, h * 64:(h + 1) * 64], Ce[:, r_, 0:BLK], False, True, r=["S_bf", kce], w=[pyk],
                           inc=(r_ == 3), tp=tp)
                for jj in range(2):
                    j = 2 * g + jj
                    stt("dve", yT[:, j, blk], xsT[:, j, blk], P(l, "dsk")[:, j:j + 1],
                        ps[pi][:, jj * 128:jj * 128 + BLK], ALU.mult, ALU.add, r=[("xsT", j), "par", pyk],
                        w=[("yT", j)])

            G1(0)
            for g in range(8):
                if g + 1 < 8:
                    G1(g + 1)
                G2(g)
        if dd:
            dump("y0", yT[:, :, :], [("yT", j) for j in range(16)])
            dump("S", S[:, :], [SK])
        if is_meta:
            dma(smeta_d[li], S[:, :], r=[SK], w=[("smeta", li)], dsem="d_smo%d" % li)
        fence(ALLSCR)
        def zB2(jq):
            for pp in range(2):
                j0 = 4 * jq + 2 * pp
                pi_ = 2 + pp
                mm(ps[pi_][:, :Tn], ones_b, sqy[2 * pp][:, :Tn], True, False, r=[("sqy", 2 * pp), "cb"],
                   w=[("ps", pi_)])
                mm(ps[pi_][:, :Tn], ones_b, sqy[2 * pp + 1][:, :Tn], False, True, r=[("sqy", 2 * pp + 1), "cb"],
                   w=[("ps", pi_)])
            for pp in range(2):
                act(rg2[pp][:, :Tn], ps[2 + pp][:, :Tn], AF.Ln, r=[("ps", 2 + pp)], w=[("rg", pp)], bias=EPS,
                    scale=1.0 / 256)
            for pp in range(2):
                act(rg2[pp][:, :Tn], rg2[pp][:, :Tn], AF.Exp, r=[("rg", pp)], w=[("rg", pp)], scale=-0.5)
            for pp in range(2):
                j0 = 4 * jq + 2 * pp
                for j2 in (j0, j0 + 1):
                    stt("dve", yT[:, j2, :Tn], yT[:, j2, :Tn], P(l, "ng")[:, j2:j2 + 1], rg2[pp][:, :Tn], ALU.mult,
                        ALU.mult, r=[("yT", j2), ("rg", pp), "par"], w=[("yT", j2)])

        for i in range(4):
            wt, wk = wget(("in", l, OZ + 512 * i, 512))
            wv = wt[:, :].rearrange("p (k f) -> p k f", k=8)
            for jj in range(4):
                j = 4 * i + jj
                o, pk = proj_chunk(wv, wk, jj * 128, Tn)
                z_ = zs[j % 2]
                act(z_[:, :Tn], o, AF.Silu, r=[pk], w=[("zs", j % 2)])
                tt("pool", yT[:, j, :Tn], yT[:, j, :Tn], z_[:, :Tn], ALU.mult, r=[("yT", j), ("zs", j % 2)],
                   w=[("yT", j)])
                tt("dve", sqy[jj][:, :Tn], yT[:, j, :Tn], yT[:, j, :Tn], ALU.mult, r=[("yT", j)], w=[("sqy", jj)])
            zB2(i)
        if dd:
            dump("yn", yT[:, :, :], [("yT", j) for j in range(16)])
        fence(ALLSCR + BIGA + BIGB)
        if not is_meta:
            T.op("pool", lambda e: e.memset(kc_t[64:128, :, :], 0.0), r=[], w=["kc_t"])
            T.op("pool", lambda e: e.memset(kc_b[0:64, :, :], 0.0), r=[], w=["kc_b"])
        posl = slice(pos0, pos0 + Tn)

        nrc = {"i": 0}
        nr_pend = []

        def nr_s1(o, pk, gcol, dsts):
            i = nrc["i"] % 2
            nrc["i"] += 1
            qf, qsq, qr = qf2[i], qsq2[i], qr2[i]
            act(qf[:, :Tn], o, AF.Copy, r=[pk], w=[("qf", i)])
            act(qsq[:, :Tn], o, AF.Square, r=[pk], w=[("qsq", i)])
            mm(ps[2][:, :Tn], bones_b, qsq[:, :Tn], True, True, r=[("qsq", i), "cb"], w=[("ps", 2)])
            rsq(qr[:, :Tn], ps[2][:, :Tn], 64, [("ps", 2)], ("qr", i))
            nr_pend.append((i, gcol, dsts))

        def nr_s2():
            i, gcol, dsts = nr_pend.pop(0)
            qf, qr, qn, t1, t2 = qf2[i], qr2[i], qn2[i], t12[i], t22[i]
            stt("dve", qn[:, :Tn], qf[:, :Tn], gcol, qr[:, :Tn], ALU.mult, ALU.mult,
                r=[("qf", i), ("qr", i), "par"], w=[("qn", i)])
            mm(ps[3][:, :Tn], prot_b, qn[:, :Tn], True, True, r=[("qn", i), "cb"], w=[("ps", 3)])
            tt("pool", t1[:, :Tn], qn[:, :Tn], cosT[:, posl], ALU.mult, r=[("qn", i), "cb"], w=[("t1", i)])
            tt("dve", t2[:, :Tn], ps[3][:, :Tn], sinT[:, posl], ALU.mult, r=[("ps", 3), "cb"], w=[("t2", i)])
            for (psl, dst, dk) in dsts:
                tt("pool", dst, t1[psl, :Tn], t2[psl, :Tn], ALU.add, r=[("t1", i), ("t2", i)], w=[dk])

        def norm_rope(o, pk, gcol, dsts):
            nr_s1(o, pk, gcol, dsts)
            if len(nr_pend) > 1:
                nr_s2()

        for i in range(2):
            wt, wk = wget(("in", l, OQ + 512 * i, 512))
            wv = wt[:, :].rearrange("p (k f) -> p k f", k=8)
            for jj in range(4):
                cq = 4 * i + jj
                o, pk = proj_chunk(wv, wk, jj * 128, Tn)
                norm_rope(o, pk, P(l, "qg")[:, 0:1], [(slice(0, 128), qT[:, cq, :Tn], ("qT", cq))])
        wt, wk = wget(("in", l, OKK, 256))
        wv = wt[:, 0:2048].rearrange("p (k f) -> p k f", k=8)
        for j in range(4):
            pt, pk = dense_ps()
            for kc in range(8):
                mm(pt[0:64, :Tn], wv[:, kc, j * 64:(j + 1) * 64], xnT[:, kc, :Tn], kc == 0, kc == 7, r=[wk] + XN,
                   w=[pk], inc=False)
            for kc in range(8):
                mm(pt[64:128, :Tn], wv[:, kc, j * 64:(j + 1) * 64], xnT[:, kc, :Tn], kc == 0, kc == 7, r=[wk] + XN,
                   w=[pk], inc=(kc == 7), tp=(0, 64))
            if is_meta:
                d_t, d_b = kmeta_t[li][0:64, j, 0:Tn], kmeta_b[li][64:128, j, 0:Tn]
                kt_, kb_ = ("kmt", li), ("kmb", li)
            else:
                d_t, d_b = kc_t[0:64, j, :Tn], kc_b[64:128, j, :Tn]
                kt_, kb_ = "kc_t", "kc_b"
            norm_rope(pt[:, :Tn], pk, P(l, "kg")[:, 0:1], [(slice(0, 64), d_t, kt_), (slice(64, 128), d_b, kb_)])
        while nr_pend:
            nr_s2()
        wt, wk = wget(("in", l, OV, 256))
        wv = wt[:, 0:2048].rearrange("p (k f) -> p k f", k=8)
        for b in range(NB):
            pt, pk = dense_ps()
            for kc in range(8):
                mm(pt[0:BLK, 0:256], xnT[:, kc, b * BLK:(b + 1) * BLK], wv[:, kc, :], kc == 0, kc == 7,
                   r=[wk] + XN, w=[pk])
            if is_meta:
                act(vmeta[li][0:BLK, :], pt[0:BLK, 0:256], AF.Copy, r=[pk], w=[("vm", li)])
            else:
                act(Vt[0:BLK, b, :], pt[0:BLK, 0:256], AF.Copy, r=[pk], w=[("Vt", b)])
        if dd:
            dump("qT", qT[:, :, :], BIGB)
            dump("kc_t", kc_t[:, :, :], BIGB)
            dump("kc_b", kc_b[:, :, :], BIGB)
            dump("Vt", Vt[:, :, :], BIGB)
        QK = [("qT", cq) for cq in range(8)]
        def cq_(ap2d):
            return ap2d.rearrange("p (c q) -> p c q", c=2)[:, :, 0:BLK]

        def A1(b, j):
            blk = slice(b * BLK, (b + 1) * BLK)
            jp_ = j % 2
            Pown, Pprev, Pmeta, den = Pown2[jp_], Pprev2[jp_], Pmeta2[jp_], den2[jp_]
            io, ipv, ime, i7 = (4, 5, 6, 7) if jp_ == 0 else (1, 2, 3, 0)
            parts = []
            if is_meta:
                parts.append(("own", io, BLK, kmeta_t[li][:, j, 0:BLK], kmeta_b[li][:, j, 0:BLK],
                              vmeta[li][0:BLK, j * 64:(j + 1) * 64], [("kmt", li), ("kmb", li), ("vm", li)], Pown,
                              tri_b))
            else:
                parts.append(("own", io, BLK, kc_t[:, j, blk], kc_b[:, j, blk], Vt[0:BLK, b, j * 64:(j + 1) * 64],
                              ["kc_t", "kc_b", ("Vt", b)], Pown, tri_b))
                if b > 0:
                    pb = slice((b - 1) * BLK, b * BLK)
                    parts.append(("prev", ipv, BLK, kc_t[:, j, pb], kc_b[:, j, pb],
                                  Vt[0:BLK, b - 1, j * 64:(j + 1) * 64], ["kc_t", "kc_b", ("Vt", b - 1)], Pprev,
                                  strict_b))
                elif not first_real:
                    parts.append(("prev", ipv, 128, kprev_t[li][:, j, :], kprev_b[li][:, j, :],
                                  vprev[li][:, j * 64:(j + 1) * 64], [("kpt", li), ("kpb", li), ("vp", li)], Pprev,
                                  strict_b))
                parts.append(("meta", ime, 16, kmeta_t[li][:, j, :], kmeta_b[li][:, j, :],
                              vmeta[li][0:16, j * 64:(j + 1) * 64], [("kmt", li), ("kmb", li), ("vm", li)], Pmeta,
                              None))
            rhs_q = qT[:, 2 * j:2 * j + 2, blk]
            for (nm, pi, KR, lt_, lb_, vap, keys, Pb, mask) in parts:
                pk = ("ps", pi)
                mm(cq_(ps[pi][0:KR, 0:256]), lt_, rhs_q, True, True, r=keys + QK, w=[pk], inc=False)
                mm(cq_(ps[pi][0:KR, 256:512]), lb_, rhs_q, True, True, r=keys + QK, w=[pk], inc=True)
                pv = ps[pi][0:KR, :].rearrange("p (a q) -> p a q", a=4)[:, :, 0:BLK]
                Pm = Pb[0:KR, :].rearrange("p (a q) -> p a q", a=4)[:, :, 0:BLK]
                act(Pm, pv, AF.Exp, r=[pk], w=[(("P", nm), jp_)], scale=0.125)
                if mask is not None:
                    tt("pool", Pm, Pm, mask[0:KR, 0:BLK].unsqueeze(1).to_broadcast([KR, 4, BLK]), ALU.mult,
                       r=[(("P", nm), jp_), "cb"], w=[(("P", nm), jp_)])
            return (b, j, blk, jp_, parts, den, i7)

        def A2(ctx):
            (b, j, blk, jp_, parts, den, i7) = ctx
            pk7 = ("ps", i7)
            npart = len(parts)
            for s in range(2):
                tp = (0, 64) if s else None
                for ip, (nm, pi, KR, lt_, lb_, vap, keys, Pb, mask) in enumerate(parts):
                    rhs = cq_(Pb[0:KR, s * 256:(s + 1) * 256])
                    mm(cq_(ps[i7][s * 64:(s + 1) * 64, 0:256]), vap, rhs, ip == 0, ip == npart - 1,
                       r=keys + [(("P", nm), jp_)], w=[pk7], inc=False, tp=tp)
                for ip, (nm, pi, KR, lt_, lb_, vap, keys, Pb, mask) in enumerate(parts):
                    rhs = cq_(Pb[0:KR, s * 256:(s + 1) * 256])
                    mm(cq_(ps[i7][s * 64:(s + 1) * 64, 256:512]), ones_b[0:KR, 0:64], rhs, ip == 0,
                       ip == npart - 1, r=[(("P", nm), jp_), "cb"], w=[pk7], inc=(s == 1 and ip == npart - 1), tp=tp)
            dv = cq_(den[:, 0:256])
            tt("dve", dv, cq_(ps[i7][:, 256:512]),
               esk[:, l, 2 * j:2 * j + 2].unsqueeze(2).to_broadcast([128, 2, BLK]), ALU.add,
               r=[pk7, "derived"], w=[("den", jp_)])
            T.op("dve", lambda e, dv=dv: e.reciprocal(out=dv, in_=dv), r=[("den", jp_)], w=[("den", jp_)])
            tt("dve", yaT[:, 2 * j:2 * j + 2, blk], cq_(ps[i7][:, 0:256]), dv,
               ALU.mult, r=[pk7, ("den", jp_)], w=[("yaT", 2 * j), ("yaT", 2 * j + 1)])

        items = [(b, j) for b in range(NB) for j in range(4)]
        ctxs = []
        for it in items:
            ctxs.append(A1(*it))
            if len(ctxs) > 1:
                A2(ctxs.pop(0))
        while ctxs:
            A2(ctxs.pop(0))
        if not is_meta:
            cp("pool", kprev_t[li][0:64, :, :], kc_t[0:64, :, Tn - 128:Tn], r=["kc_t"], w=[("kpt", li)])
            cp("pool", kprev_b[li][64:128, :, :], kc_b[64:128, :, Tn - 128:Tn], r=["kc_b"], w=[("kpb", li)])
            cp("pool", vprev[li][:, :], Vt[:, NB - 1, :], r=[("Vt", NB - 1)], w=[("vp", li)])
        if dd:
            dump("yaT", yaT[:, :, :], BIGB)
        fence(ALLSCR)
        YN = [("yT", j) for j in range(16)]
        YA = [("yaT", j) for j in range(8)]
        for h in range(2):
            wgs, kgs = wget(("in", l, OG + 512 * h, 512))
            for jj in range(4):
                oc = 4 * h + jj
                o, pk = proj_chunk(wgs[:, :].rearrange("p (k f) -> p k f", k=8), kgs, jj * 128, Tn)
                act(gs_sb[:, jj, :Tn], o, AF.Sigmoid, r=[pk, "par"], w=[("gs", jj)], bias=P(l, "bg")[:, oc:oc + 1])
            wga, kga = wget(("in", l, OG + 1024 + 512 * h, 512))
            for jj in range(4):
                oc = 4 * h + jj
                o, pk = proj_chunk(wga[:, :].rearrange("p (k f) -> p k f", k=8), kga, jj * 128, Tn)
                act(ga_sb[:, jj, :Tn], o, AF.Sigmoid, r=[pk, "par"], w=[("ga", jj)],
                    bias=P(l, "bg")[:, 8 + oc:9 + oc])
            wad, kad = wget(("ad", l, h), hold=True)
            wadv = wad[:, :].rearrange("p (k f) -> p k f", k=8)
            for jp in range(2):
                wsd, ksd = wget(("sd", l, 2 * h + jp))
                wsdv = wsd[:, :].rearrange("p (k f) -> p k f", k=16)
                for j2 in range(2):
                    jj = 2 * jp + j2
                    oc = 4 * h + jj
                    pa, pka = ps[2], ("ps", 2)
                    for kc in range(16):
                        mm(pa[:, :Tn], wsdv[:, kc, j2 * 128:(j2 + 1) * 128], yT[:, kc, :Tn], kc == 0, kc == 15,
                           r=[ksd] + YN, w=[pka])
                    pb_, pkb = ps[3], ("ps", 3)
                    for kc in range(8):
                        mm(pb_[:, :Tn], wadv[:, kc, jj * 128:(jj + 1) * 128], yaT[:, kc, :Tn], kc == 0, kc == 7,
                           r=[kad] + YA, w=[pkb])
                    tt("dve", m1[:, :Tn], pa[:, :Tn], gs_sb[:, jj, :Tn], ALU.mult, r=[pka, ("gs", jj)], w=["m1"])
                    tt("dve", m2[:, :Tn], pb_[:, :Tn], ga_sb[:, jj, :Tn], ALU.mult, r=[pkb, ("ga", jj)], w=["m2"])
                    tt("pool", mT[:, oc, :Tn], m1[:, :Tn], m2[:, :Tn], ALU.add, r=["m1", "m2"], w=[("mT", oc)])
            wunhold()
        if dd:
            dump("mT", mT[:, :, :], [("mT", j) for j in range(8)])
        MT = [("mT", j) for j in range(8)]
        for h in range(2):
            wt, wk = wget(("o", l, h))
            wv = wt[:, :].rearrange("p (k f) -> p k f", k=8)
            for jj in range(4):
                oc = 4 * h + jj
                pt, pk = dense_ps()
                for kc in range(8):
                    mm(pt[:, :Tn], wv[:, kc, jj * 128:(jj + 1) * 128], mT[:, kc, :Tn], kc == 0, kc == 7, r=[wk] + MT,
                       w=[pk])
                tt("dve", hT[:, oc, :Tn], hT[:, oc, :Tn], pt[:, :Tn], ALU.add, r=["hT", pk], w=["hT"])
        if dd:
            dump("h1", hT[:, :, :], ["hT"])
        fence(ALLSCR + BIGB + BIGC)
        rmsnorm_tile(l, "g2", Tn)
        for i in range(8):
            wt, wk = wget(("up", l, i))
            wv = wt[:, :].rearrange("p (k f) -> p k f", k=8)
            for jj in range(4):
                j = 4 * i + jj
                o, pk = proj_chunk(wv, wk, jj * 128, Tn)
                r_ = rl[j % 2]
                act(r_[:, :Tn], o, AF.Relu, r=[pk], w=[("rl", j % 2)])
                tt("pool", hid[:, j, :Tn], r_[:, :Tn], r_[:, :Tn], ALU.mult, r=[("rl", j % 2)], w=[("hid", j)])
        HID = [("hid", j) for j in range(32)]
        for oc in range(8):
            wt, wk = wget(("down", l, oc))
            wv = wt[:, :].rearrange("p (k f) -> p k f", k=32)
            pt, pk = dense_ps()
            for kc in range(32):
                mm(pt[:, :Tn], wv[:, kc, :], hid[:, kc, :Tn], kc == 0, kc == 31, r=[wk] + HID, w=[pk])
            tt("dve", hT[:, oc, :Tn], hT[:, oc, :Tn], pt[:, :Tn], ALU.add, r=["hT", pk], w=["hT"])


    def fence(keys):
        T.op("pool", lambda e: e.memset(fz[0:1, 0:1], 0.0), r=[], w=list(keys))

    for (s_, t_) in tiles:
        is_meta = (t_ < 0)
        if is_meta:
            Tn, BLK, NB, pos0 = NMETA, NMETA, 1, 0
        else:
            Tn, BLK, NB, pos0 = TT, 128, 4, NMETA + TT * t_
        if s_ > 0 and t_ == 0:
            for li in range(NL):
                dma(Sst[li][:, :], smeta_d[li], r=[("smeta", li)], w=[("S", li)], dsem="d_smi%d" % li)
                cp("pool", tails[li][:, :, :], tails_m[li][:, :, :], r=[("tailm", li)], w=[("tail", li)])
        fence(ALLSCR)
        for b in range(NB):
            st = stage[b % 2]
            sk = ("stage", b % 2)
            if is_meta:
                src = meta_d[:, :]
            else:
                src = x_d[s_, t_ * TT + b * 128:t_ * TT + (b + 1) * 128, :]
            dma(st[0:BLK, :], src, r=[], w=[sk], dsem="d_stage%d" % (b % 2))
            for half in range(2):
                pi = 2 + half
                for kk in range(4):
                    kc = half * 4 + kk
                    tr(ps[pi][:, kk * 128:kk * 128 + BLK], st[0:BLK, kc * 128:(kc + 1) * 128], ident_f[0:BLK, 0:BLK],
                       r=[sk, "cf"], w=[("ps", pi)], inc=(kk == 3))
                cp("act" if half else "dve", hT[:, half * 4:half * 4 + 4, b * BLK:(b + 1) * BLK],
                   ps[pi][:, :].rearrange("p (k t) -> p k t", k=4)[:, :, 0:BLK], r=[("ps", pi)], w=["hT"])
        for li, l in enumerate(layers):
            layer(li, l, Tn, BLK, NB, is_meta, t_, pos0)
        fence(ALLSCR)
        if not is_meta:
            for b in range(NB):
                st = stage[b % 2]
                sk = ("stage", b % 2)
                for half in range(2):
                    pi = 2 + half
                    for kk in range(4):
                        kc = half * 4 + kk
                        tr(ps[pi][:, kk * 128:(kk + 1) * 128], hT[:, kc, b * 128:(b + 1) * 128], ident_f, r=["hT", "cf"],
                           w=[("ps", pi)], inc=(kk == 3))
                    cp("act" if half else "dve", st[:, half * 512:(half + 1) * 512], ps[pi][:, :], r=[("ps", pi)],
                       w=[sk])
                dma(out_d[s_, t_ * TT + b * 128:t_ * TT + (b + 1) * 128, :], st[:, :], r=[sk], w=[("out", s_, t_, b)],
                    dsem="d_out%d" % (b % 2))
    for k in ("d_out0", "d_out1", "d_dbg"):
        if k in T.cnt:
            T.prog["sp"].append(("wait", k, T.cnt[k]))

    for k in sorted(T.semkeys):
        T.semh[k] = es.enter_context(nc.semaphore("s_" + k))
    with nc.Block() as block:
        @block.tensor
        def _(e):
            T.replay("pe", e)

        @block.scalar
        def _(e):
            T.replay("act", e)

        @block.vector
        def _(e):
            T.replay("dve", e)

        @block.gpsimd
        def _(e):
            T.replay("pool", e)

        @block.sync
        def _(e):
            T.replay("sp", e)
    es.close()
    nc._dbg_names = list(dbgs.keys())
    return nc, T


def host_consts():
    k = np.arange(128)
    tri = (k[:, None] <= k[None, :]).astype(np.float32)
    strict = (k[:, None] > k[None, :]).astype(np.float32)
    ones = np.ones((128, 128), np.float32)
    ident = np.eye(128, dtype=np.float32)
    bones = np.zeros((128, 128), np.float32)
    bones[:64, :64] = 1
    bones[64:, 64:] = 1
    prot = np.zeros((128, 128), np.float32)
    for blk in (0, 64):
        for d in range(32):
            prot[blk + d + 32, blk + d] = -1.0
            prot[blk + d, blk + d + 32] = 1.0
    half = 32
    inv_freq = (np.float32(10000.0) ** (-np.arange(half, dtype=np.float32) / np.float32(half))).astype(np.float32)
    pos = np.arange(NPOS).astype(np.float32)
    ang = (pos[:, None] * inv_freq[None, :]).astype(np.float32)
    cos = np.cos(ang).astype(np.float32)
    sin = np.sin(ang).astype(np.float32)
    p = np.arange(128) % 32
    cosT = cos[:, p].T
    sinT = sin[:, p].T
    cf = np.concatenate([tri, strict, ones, ident], axis=1)
    cbf = np.concatenate([tri, strict, ones, ident, bones, prot, cosT, sinT], axis=1)
    return np.ascontiguousarray(cf, np.float32), np.ascontiguousarray(cbf, np.float32)


def host_params(inp):
    par = np.zeros((DEPTH, 128, NPC), np.float32)

    def put(name, arr):
        o, w = PC[name]
        par[:, :, o:o + w] = arr

    fm = lambda a, n: a.reshape(DEPTH, n, 128).transpose(0, 2, 1)
    put("g1", fm(inp["norm1_g"], 8))
    put("g2", fm(inp["norm2_g"], 8))
    cw = inp["conv_w"].reshape(DEPTH, 4, 32, 128).transpose(0, 3, 2, 1).reshape(DEPTH, 128, 128)
    put("cw", cw)
    put("cb", fm(inp["conv_b"], 32))
    put("bg", fm(inp["b_gate"], 16))
    put("ng", fm(inp["ssd_norm_g"], 16))
    put("dsk", fm(np.repeat(inp["d_skip"], 64, axis=1), 16))
    put("qg", np.tile(inp["q_norm_g"], (1, 2))[:, :, None])
    put("kg", np.tile(inp["k_norm_g"], (1, 2))[:, :, None])
    sk = inp["sinks"].reshape(DEPTH, 8, 2)
    put("sink", np.repeat(sk.transpose(0, 2, 1), 64, axis=1))
    put("dtb", np.broadcast_to(inp["dt_bias"][:, None, :], (DEPTH, 128, 32)))
    put("alog", np.broadcast_to(inp["a_log"][:, None, :], (DEPTH, 128, 32)))
    return par


_PROG = {}


def run(inputs, nseq_per_core, ncores, layers, trace=False, dbg=False):
    inp = {k: np.asarray(v) for k, v in inputs.items()}
    key = (nseq_per_core, tuple(layers))
    if key not in _PROG:
        _PROG[key] = build_program(nseq_per_core, layers, dbg=dbg)
    nc, _ = _PROG[key]
    cf, cbf = host_consts()
    par = host_params(inp)
    x = np.ascontiguousarray(inp["x"], np.float32)
    shared = {
        "meta": np.ascontiguousarray(inp["meta_tokens"], np.float32), "params": par, "cf32": cf, "cbf": cbf,
        "w_in": np.ascontiguousarray(inp["w_in"], np.float32), "w_sd": np.ascontiguousarray(inp["w_ssd_down"], np.float32),
        "w_ad": np.ascontiguousarray(inp["w_attn_down"], np.float32), "w_o": np.ascontiguousarray(inp["w_o"], np.float32),
        "w_up": np.ascontiguousarray(inp["w_mlp_up"], np.float32),
        "w_down": np.ascontiguousarray(inp["w_mlp_down"], np.float32),
    }
    in_maps = []
    for c_ in range(ncores):
        m = dict(shared)
        m["x"] = x[c_ * nseq_per_core:(c_ + 1) * nseq_per_core]
        in_maps.append(m)
    res = run_bass_kernel_spmd(nc, in_maps, core_ids=list(range(ncores)), trace=trace)
    out = np.concatenate([np.asarray(r["out"]) for r in res.results], axis=0)
    return out.astype(np.float32), res


def kernel(**inputs):
    out, _ = run(inputs, 32 // NCORES, NCORES, list(range(DEPTH)))
    return out
```
